# Optimizing a Trainium2 kernel written in Bass

```python
import math
import jax, jax.numpy as jnp
from jax import lax
import numpy as np

D_MODEL = 1024
BATCH = 8
SEQ = 2048
DEPTH = 1

DA_HEADS = 4
DA_HEAD_DIM = 64
DA_V_DIM = 2 * DA_HEAD_DIM
ROPE_THETA = 500000.0
ROPE_DIM = DA_HEAD_DIM // 4
Q_BLOCK = 128
LAMBDA_STD = 0.1
RET_HEADS = 4
RET_QK_DIM = 64
RET_V_DIM = 128
RET_CHUNK = 128
RET_ROT_BASE = 10000.0
DA_QK_W = DA_HEADS * 2 * DA_HEAD_DIM
DA_V_W = DA_HEADS * DA_V_DIM
RET_QK_W = RET_HEADS * RET_QK_DIM
RET_V_W = RET_HEADS * RET_V_DIM
IN_SPLITS = (DA_QK_W, DA_QK_W, DA_V_W, RET_QK_W, RET_QK_W, RET_V_W, RET_V_W, D_MODEL, D_MODEL)
IN_W = sum(IN_SPLITS)
D_FF = -(-8 * D_MODEL // (3 * 256)) * 256
EPS = 1e-6

kernel_name = "hybrid_diffattn_retention_gated_block"


def rms_norm(x, g=None):
    xf = x.astype(jnp.float32)
    y = xf * lax.rsqrt(jnp.mean(xf * xf, axis=-1, keepdims=True) + EPS)
    if g is not None:
        y = y * g.astype(jnp.float32)
    return y.astype(x.dtype)


def rotary(x, pos, rot_dim, theta):
    half = rot_dim // 2
    inv_freq = theta ** (-jnp.arange(half, dtype=jnp.float32) / half)
    ang = pos.astype(jnp.float32)[..., None] * inv_freq
    cos = jnp.cos(ang)[:, :, None, :]
    sin = jnp.sin(ang)[:, :, None, :]
    xr = x[..., :rot_dim].astype(jnp.float32)
    x1, x2 = xr[..., :half], xr[..., half:]
    rot = jnp.concatenate([x1 * cos - x2 * sin, x2 * cos + x1 * sin], axis=-1).astype(x.dtype)
    return jnp.concatenate([rot, x[..., rot_dim:]], axis=-1)


def diff_attention(q, k, v, pos, qn_g, kn_g, lq1, lk1, lq2, lk2, subln_g, lambda_init):
    B, S, H2, d = q.shape
    H = H2 // 2
    q = rotary(rms_norm(q, qn_g), pos, ROPE_DIM, ROPE_THETA)
    k = rotary(rms_norm(k, kn_g), pos, ROPE_DIM, ROPE_THETA)
    f32 = jnp.float32
    lam = (jnp.exp(jnp.sum(lq1.astype(f32) * lk1.astype(f32)))
           - jnp.exp(jnp.sum(lq2.astype(f32) * lk2.astype(f32))) + lambda_init)
    nb = S // Q_BLOCK
    qb = q.reshape(B, nb, Q_BLOCK, H2, d).transpose(1, 0, 3, 2, 4)
    kt = k.transpose(0, 2, 1, 3)
    vt = v.transpose(0, 2, 1, 3)
    scale = d ** -0.5
    kpos = jnp.arange(S)

    def block(args):
        qblk, i = args
        s = jnp.einsum('bhqd,bhkd->bhqk', qblk, kt).astype(f32) * scale
        qpos = i * Q_BLOCK + jnp.arange(Q_BLOCK)
        s = jnp.where(kpos[None, :] <= qpos[:, None], s, -1e30)
        p = jax.nn.softmax(s, axis=-1).reshape(B, H, 2, Q_BLOCK, S)
        a = (p[:, :, 0] - lam * p[:, :, 1]).astype(vt.dtype)
        return jnp.einsum('bhqk,bhkd->bhqd', a, vt)

    o = lax.map(block, (qb, jnp.arange(nb)))
    o = o.transpose(1, 0, 3, 2, 4).reshape(B, S, H, 2 * d)
    return rms_norm(o, subln_g) * (1.0 - lambda_init)


def retention(q, k, v, pos):
    B, S, H, dk = q.shape
    dv = v.shape[-1]
    C = RET_CHUNK
    N = S // C
    f32 = jnp.float32
    q = rotary(q, pos, dk, RET_ROT_BASE)
    k = rotary(k, pos, dk, RET_ROT_BASE)
    qc = q.astype(f32).reshape(B, N, C, H, dk)
    kc = k.astype(f32).reshape(B, N, C, H, dk) * (dk ** -0.5)
    vc = v.astype(f32).reshape(B, N, C, H, dv)
    log_g = jnp.log(1.0 - 2.0 ** (-5.0 - jnp.arange(H, dtype=f32)))
    idx = jnp.arange(C, dtype=f32)
    rel = idx[:, None] - idx[None, :]
    dmask = jnp.where(rel >= 0, jnp.exp(log_g[:, None, None] * jnp.maximum(rel, 0.0)), 0.0)
    sc = jnp.einsum('bnihd,bnjhd->bnhij', qc, kc) * dmask
    inner = jnp.einsum('bnhij,bnjhe->bnihe', sc, vc)
    k_decay = jnp.exp(log_g[:, None] * (C - 1.0 - idx)[None, :])
    kv = jnp.einsum('bnjhd,hj,bnjhe->nbhde', kc, k_decay, vc)
    chunk_decay = jnp.exp(log_g * C)[None, :, None, None]

    def step(R, kv_n):
        return R * chunk_decay + kv_n, R

    _, R_prev = lax.scan(step, jnp.zeros((B, H, dk, dv), f32), kv)
    q_decay = jnp.exp(log_g[:, None] * (idx + 1.0)[None, :])
    cross = jnp.einsum('bnihd,nbhde,hi->bnihe', qc, R_prev, q_decay)
    o = (inner + cross).reshape(B, S, H, dv)
    return rms_norm(o).astype(v.dtype)


def setup_inputs(seed: int = 0) -> dict:
    key = jax.random.key(seed)
    ks = jax.random.split(key, 24)
    n = jax.random.normal
    f = jnp.float32
    x = n(ks[0], (BATCH, SEQ, D_MODEL), f)
    c = n(ks[1], (BATCH, D_MODEL), f)
    offset = jax.random.randint(ks[2], (BATCH, 1), 0, 4096, dtype=jnp.int32)
    positions = (offset + jnp.arange(SEQ, dtype=jnp.int32)[None, :]).astype(jnp.int32)
    return {
        "x": x,
        "c": c,
        "positions": positions,
        "ada_w": n(ks[3], (DEPTH, D_MODEL, 6 * D_MODEL), f) * D_MODEL ** -0.5,
        "ada_b": n(ks[4], (DEPTH, 6 * D_MODEL), f) * 0.02,
        "norm1_g": 1.0 + 0.02 * n(ks[5], (DEPTH, D_MODEL), f),
        "w_in": n(ks[6], (DEPTH, D_MODEL, IN_W), f) * D_MODEL ** -0.5,
        "q_norm_g": 1.0 + 0.02 * n(ks[7], (DEPTH, DA_HEAD_DIM), f),
        "k_norm_g": 1.0 + 0.02 * n(ks[8], (DEPTH, DA_HEAD_DIM), f),
        "lambda_q1": n(ks[9], (DEPTH, DA_HEAD_DIM), f) * LAMBDA_STD,
        "lambda_k1": n(ks[10], (DEPTH, DA_HEAD_DIM), f) * LAMBDA_STD,
        "lambda_q2": n(ks[11], (DEPTH, DA_HEAD_DIM), f) * LAMBDA_STD,
        "lambda_k2": n(ks[12], (DEPTH, DA_HEAD_DIM), f) * LAMBDA_STD,
        "subln_g": 1.0 + 0.02 * n(ks[13], (DEPTH, DA_V_DIM), f),
        "w_branch_a": n(ks[14], (DEPTH, DA_V_W, D_MODEL), f) * DA_V_W ** -0.5,
        "w_branch_b": n(ks[15], (DEPTH, RET_V_W, D_MODEL), f) * RET_V_W ** -0.5,
        "w_out": n(ks[16], (DEPTH, D_MODEL, D_MODEL), f) * D_MODEL ** -0.5,
        "norm2_g": 1.0 + 0.02 * n(ks[17], (DEPTH, D_MODEL), f),
        "w_gate_up": n(ks[18], (DEPTH, D_MODEL, 2 * D_FF), f) * D_MODEL ** -0.5,
        "w_down": n(ks[19], (DEPTH, D_FF, D_MODEL), f) * D_FF ** -0.5,
    }


def reference(x, c, positions, ada_w, ada_b, norm1_g, w_in, q_norm_g, k_norm_g,
              lambda_q1, lambda_k1, lambda_q2, lambda_k2, subln_g, w_branch_a, w_branch_b,
              w_out, norm2_g, w_gate_up, w_down):
    B, S, _ = x.shape
    splits = [int(s) for s in np.cumsum(IN_SPLITS)[:-1]]
    for l in range(DEPTH):
        mod = jnp.einsum('bd,de->be', jax.nn.silu(c), ada_w[l]) + ada_b[l]
        sh1, sc1, g1, sh2, sc2, g2 = jnp.split(mod, 6, axis=-1)
        h = rms_norm(x, norm1_g[l]) * (1.0 + sc1[:, None, :]) + sh1[:, None, :]
        proj = jnp.einsum('bsd,de->bse', h, w_in[l])
        dq, dk, dv, rq, rk, rv, rg, ga, gb = jnp.split(proj, splits, axis=-1)
        lambda_init = 0.8 - 0.6 * math.exp(-0.3 * l)
        oa = diff_attention(
            dq.reshape(B, S, 2 * DA_HEADS, DA_HEAD_DIM),
            dk.reshape(B, S, 2 * DA_HEADS, DA_HEAD_DIM),
            dv.reshape(B, S, DA_HEADS, DA_V_DIM),
            positions, q_norm_g[l], k_norm_g[l],
            lambda_q1[l], lambda_k1[l], lambda_q2[l], lambda_k2[l], subln_g[l], lambda_init)
        ob = retention(
            rq.reshape(B, S, RET_HEADS, RET_QK_DIM),
            rk.reshape(B, S, RET_HEADS, RET_QK_DIM),
            rv.reshape(B, S, RET_HEADS, RET_V_DIM),
            positions)
        ob = ob.reshape(B, S, RET_V_W) * jax.nn.silu(rg)
        ya = jnp.einsum('bse,ed->bsd', oa.reshape(B, S, DA_V_W), w_branch_a[l])
        yb = jnp.einsum('bse,ed->bsd', ob, w_branch_b[l])
        y = jax.nn.sigmoid(ga) * ya + jax.nn.sigmoid(gb) * yb
        x = x + g1[:, None, :] * jnp.einsum('bsd,de->bse', y, w_out[l])
        h2 = rms_norm(x, norm2_g[l]) * (1.0 + sc2[:, None, :]) + sh2[:, None, :]
        gate, up = jnp.split(jnp.einsum('bsd,de->bse', h2, w_gate_up[l]), 2, axis=-1)
        x = x + g2[:, None, :] * jnp.einsum('bsf,fd->bsd', jax.nn.silu(gate) * up, w_down[l])
    return x
```

```python
import math
import numpy as np
import concourse.bass as bass
import concourse.mybir as mybir
from concourse.bass_utils import run_bass_kernel_spmd

F32 = mybir.dt.float32
BF16 = mybir.dt.bfloat16
I32 = mybir.dt.int32
AF = mybir.ActivationFunctionType
ALU = mybir.AluOpType
AX = mybir.AxisListType

S = 2048
D = 1024
NT = 16
KC = 8
DFF = 2816
NFC = 22
EPS = 1e-6
LAMBDA_INIT = 0.8 - 0.6 * math.exp(-0.3 * 0)
TWO_PI = 2.0 * math.pi
CW1 = 6.28125
CW2 = TWO_PI - CW1
NEG = -30000.0


class Buf:
    __slots__ = ("name", "w", "rs", "sem", "cnt", "psum")

    def __init__(self, name):
        self.name = name
        self.w = []
        self.rs = []
        self.sem = None
        self.cnt = 0
        self.psum = False


class Prog:
    ENGS = ("pe", "act", "dve", "pool", "sp")

    def __init__(self, nc):
        self.nc = nc
        self.ins = []

    def buf(self, name, after=()):
        b = Buf(name)
        for o in after:
            b.w = b.w + o.w
            b.rs = b.rs + o.rs
        return b

    def _rec(self, eng, fn, reads, writes, join, dma, dsem):
        idx = len(self.ins)
        deps = {}
        for b in reads:
            for i in b.w:
                deps[i] = "RAW"
            if b.psum:
                for i in b.rs:
                    if self.ins[i]["eng"] != eng and i not in deps:
                        deps[i] = "RAR"
        for b in writes:
            if not join:
                for i in b.w:
                    deps[i] = "WAW"
            for i in b.rs:
                if i not in deps:
                    deps[i] = "WAR"
        self.ins.append(dict(eng=eng, fn=fn, deps=deps, dma=dma, dsem=dsem, inc=False, val=None))
        for b in reads:
            if not dma:
                b.rs = [i for i in b.rs if self.ins[i]["dma"] or self.ins[i]["eng"] != eng]
            b.rs.append(idx)
        for b in writes:
            if join:
                b.w = b.w + [idx]
            else:
                b.w = [idx]
            b.rs = []
        return idx

    def op(self, eng, fn, reads=(), writes=(), join=False):
        return self._rec(eng, fn, reads, writes, join, False, None)

    def dma(self, eng, fn, reads=(), writes=(), sem=None, join=False):
        return self._rec(eng, fn, reads, writes, join, True, sem)

    def emit(self, block, final_wait_eng="sp"):
        nc = self.nc
        ins = self.ins
        for rec in ins:
            for i, kind in rec["deps"].items():
                pr = ins[i]
                if pr["dma"]:
                    continue
                if pr["eng"] == rec["eng"] and rec["eng"] == "pe":
                    continue
                pr["inc"] = True
        esem = {e: nc.alloc_semaphore("prog_" + e) for e in self.ENGS}
        cnt = {e: 0 for e in self.ENGS}
        for rec in ins:
            if rec["dma"]:
                b = rec["dsem"]
                if b.sem is None:
                    b.sem = nc.alloc_semaphore("dma_" + b.name)
                b.cnt += 16
                rec["val"] = (b.sem, b.cnt)
            elif rec["inc"]:
                cnt[rec["eng"]] += 1
                rec["val"] = (esem[rec["eng"]], cnt[rec["eng"]])
        last_dma = {}
        for rec in ins:
            if rec["dma"]:
                s, v = rec["val"]
                last_dma[s.num] = (s, v)
        handles = {"pe": nc.tensor, "act": nc.scalar, "dve": nc.vector, "pool": nc.gpsimd, "sp": nc.sync}
        self.nwaits = 0

        def run(eng):
            h = handles[eng]
            seen = {}
            for rec in ins:
                if rec["eng"] != eng:
                    continue
                need = {}
                for i, kind in rec["deps"].items():
                    pr = ins[i]
                    if (not pr["dma"]) and pr["eng"] == eng and eng == "pe":
                        continue
                    s, v = pr["val"]
                    if seen.get(s.num, 0) >= v:
                        continue
                    if s.num not in need or need[s.num][1] < v:
                        need[s.num] = (s, v)
                waits = list(need.values())
                for s, v in waits:
                    seen[s.num] = v
                self.nwaits += len(waits)
                for s, v in waits[:-1]:
                    h.wait_ge(s, v)
                bi = rec["fn"]()
                if waits:
                    s, v = waits[-1]
                    bi._wait_ge(s, v)
                if rec["dma"]:
                    s, v = rec["val"]
                    bi.then_inc(s, 16)
                elif rec["inc"]:
                    s, v = rec["val"]
                    bi.then_inc(s, 1)
            if eng == final_wait_eng:
                for s, v in last_dma.values():
                    if seen.get(s.num, 0) < v:
                        h.wait_ge(s, v)

        @block.tensor
        def _(e):
            run("pe")

        @block.scalar
        def _(e):
            run("act")

        @block.vector
        def _(e):
            run("dve")

        @block.gpsimd
        def _(e):
            run("pool")

        @block.sync
        def _(e):
            run("sp")


def build_nc(debug=None, stop_after=None):
    debug = debug or {}
    nc = bass.Bass("TRN2", target_bir_lowering=False)
    p = Prog(nc)

    def din(name, shape, dt=F32):
        return nc.dram_tensor(name, list(shape), dt, kind="ExternalInput").ap()

    x_d = din("x", [S, D])
    c_d = din("c", [128, KC])
    pos_d = din("pos", [128, NT], I32)
    adaw_d = din("ada_w", [D, 6 * D])
    adab_d = din("ada_b", [1, 6 * D])
    n1g_d = din("norm1_g", [1, D])
    win_d = din("w_in", [D, 5120])
    qg_d = din("q_norm_g", [1, 64])
    kg_d = din("k_norm_g", [1, 64])
    lam_d = [din(n, [1, 64]) for n in ("lambda_q1", "lambda_k1", "lambda_q2", "lambda_k2")]
    subg_d = din("subln_g", [1, 128])
    wa_d = din("w_branch_a", [512, D])
    wb_d = din("w_branch_b", [512, D])
    wout_d = din("w_out", [D, D])
    n2g_d = din("norm2_g", [1, D])
    wgu_d = din("w_gate_up", [D, 2 * DFF])
    wdn_d = din("w_down", [DFF, D])
    invf_d = din("k_invf", [1, 80])
    dec_d = din("k_dec", [128, NT * 8])
    ident_d = din("k_ident", [128, 128])
    maskb_d = din("k_maskb", [128, 128])
    mask01_d = din("k_mask01", [128, 128])
    out_d = nc.dram_tensor("out", [S, D], F32, kind="ExternalOutput").ap()
    dbg_out = {}
    for name, shape in debug.items():
        dbg_out[name] = nc.dram_tensor("dbg_" + name, list(shape), F32, kind="ExternalOutput").ap()

    def fin():
        with nc.Block() as block:
            p.emit(block)
        return nc, p

    ARENA_KB = 207
    arena = nc.alloc_sbuf_tensor("arena", [128, ARENA_KB * 512], BF16).ap()

    def V(off_b, shape, dt=BF16, parts=128):
        esz = 4 if dt in (F32, I32) else 2
        n = 1
        for s_ in shape:
            n *= s_
        nbytes = n * esz
        assert off_b % 4 == 0 and off_b + nbytes <= ARENA_KB * 1024, (off_b, nbytes)
        a = arena[0:parts, off_b // 2:(off_b + nbytes) // 2]
        if dt != BF16:
            a = a.bitcast(dt)
        if len(shape) == 2:
            a = a.rearrange("p (a b) -> p a b", b=shape[1])
        elif len(shape) == 3:
            a = a.rearrange("p (a b c) -> p a b c", b=shape[1], c=shape[2])
        return a

    KB = 1024
    hT = V(0, [KC, S])
    qT = V(32 * KB, [4, S])
    kT = V(48 * KB, [4, S])
    vA = V(64 * KB, [NT, 512])
    rqT = V(80 * KB, [2, S])
    rkT = V(88 * KB, [2, S])
    rktok = V(96 * KB, [NT, 256])
    rgsT = V(104 * KB, [4, S])
    rv = V(120 * KB, [NT, 512])
    xs = [V(136 * KB + i * 4 * KB, [D], F32) for i in range(3)]
    xn = [V(174 * KB + i * 2 * KB, [D]) for i in range(2)]
    wring = [V(148 * KB + i * 8 * KB, [KC, 512]) for i in range(3)]
    wring.append(V(120 * KB, [KC, 512]))
    oaT = V(148 * KB, [4, S])
    obT = V(32 * KB, [4, S])
    wga = V(48 * KB, [KC, D])
    wgb = V(64 * KB, [KC, D])
    yT = V(80 * KB, [KC, S])
    wab = V(120 * KB, [8, D])
    x1 = V(0, [NT, D], F32)
    woutS = V(164 * KB, [KC, D])
    actS = [V(64 * KB + i * 24 * KB, [6, S]) for i in range(2)]
    h2T = V(112 * KB, [KC, S])
    wguS = [V(144 * KB + i * 4 * KB, [KC, 256]) for i in range(3)]
    wdnS = [V(156 * KB + i * 12 * KB, [6, D]) for i in range(2)]
    tmpA = [V(172 * KB + i * 2 * KB, [512], F32) for i in range(2)]
    tmpB = [V(176 * KB + i * 2 * KB, [512], F32) for i in range(2)]
    mo = [180 * KB]

    def M(shape, dt=BF16, parts=128):
        esz = 4 if dt in (F32, I32) else 2
        n = 1
        for s_ in shape:
            n *= s_
        off = mo[0]
        mo[0] += (n * esz + 31) // 32 * 32
        assert mo[0] <= ARENA_KB * KB, mo[0]
        return V(off, shape, dt, parts)

    ident = M([128])
    maskb = M([128])
    mask01 = M([128])
    ones_bf = M([128])
    onesf = M([128], F32)
    epsb = M([1], F32)
    halfpi = M([1], F32)
    c_sb = M([KC], F32)
    cs_f = M([KC], F32)
    cs_bf = M([KC])
    cs_bc = M([KC, 128])
    pos_i = M([NT], I32)
    pos_f = M([NT], F32)
    invf = V(172 * KB + 7680, [80], F32)
    qg_bc = M([64], F32)
    kg_bc = M([64], F32)
    subg = M([1], F32)
    sg08 = M([1], F32)
    lamrow = M([256], F32, parts=1)
    lamtmp = M([8], F32, parts=1)
    nlam = M([1], F32)
    sgt_off = mo[0]
    cosT = M([NT, 40], F32)
    sinT = M([NT, 40], F32)
    angT = V(172 * KB, [NT, 40], F32)
    angN = V(172 * KB + 2560, [NT, 40], F32)
    angI = V(172 * KB + 5120, [NT, 40], I32)
    pp = M([48], F32)
    A1 = M([KC], F32)
    A2 = M([KC], F32)
    adab_pp = M([48], F32)
    ssq = M([NT], F32)
    rstd1 = M([NT], F32)
    lntmp = M([NT], F32)
    ss8 = [M([8], F32) for _ in range(2)]
    rs8 = [M([8], F32) for _ in range(2)]
    qtok = [M([512]) for _ in range(2)]
    g1bc = M([D], F32)
    g2bc = M([D], F32)
    rsl3 = [M([8], F32) for _ in range(3)]
    kzr = [[M([128]) for _ in range(3)] for _ in range(2)]
    xnB = [M([D]) for _ in range(2)]
    decg = xnB[1][:, 0:NT * 16].bitcast(F32)
    assert mo[0] <= ARENA_KB * KB
    misc_end_phaseA = mo[0]

    psum = [nc.alloc_psum_tensor(f"ps{i}", [128, 512], F32).ap() for i in range(8)]
    psum_bf = [q.bitcast(BF16) for q in psum]
    PS = [p.buf(f"ps{i}") for i in range(8)]
    for b_ in PS:
        b_.psum = True

    def mm(out, lhsT, rhs, start, stop, r, w):
        p.op("pe", lambda: nc.tensor.matmul(out, lhsT=lhsT, rhs=rhs, start=start, stop=stop), r, w)

    def tr(out, in_, r, w):
        p.op("pe", lambda: nc.tensor.transpose(out, in_, ident), r + [B_ident], w)

    def act(out, in_, func, r, w, scale=1.0, bias=None, accum=None, join=False):
        def f():
            kw = {}
            if bias is not None:
                kw["bias"] = bias
            if accum is not None:
                kw["accum_out"] = accum
            return nc.scalar.activation(out=out, in_=in_, func=func, scale=scale, **kw)
        p.op("act", f, r, w, join=join)

    def EN(eng):
        return nc.vector if eng == "dve" else nc.gpsimd

    def tt(out, in0, in1, op, r, w, eng="dve"):
        p.op(eng, lambda: EN(eng).tensor_tensor(out=out, in0=in0, in1=in1, op=op), r, w)

    def ts(out, in0, s1, s2, op0, op1, r, w, eng="dve", join=False):
        if s2 is None:
            p.op(eng, lambda: EN(eng).tensor_scalar(out=out, in0=in0, scalar1=s1, scalar2=None, op0=op0), r, w,
                 join=join)
        else:
            p.op(eng, lambda: EN(eng).tensor_scalar(out=out, in0=in0, scalar1=s1, scalar2=s2, op0=op0, op1=op1), r, w,
                 join=join)

    def stt(out, in0, scalar, in1, op0, op1, r, w, eng="dve"):
        p.op(eng, lambda: EN(eng).scalar_tensor_tensor(out=out, in0=in0, scalar=scalar, in1=in1, op0=op0, op1=op1), r, w)

    def cp(out, in_, r, w, eng="dve"):
        if eng == "act":
            p.op("act", lambda: nc.scalar.copy(out=out, in_=in_), r, w)
        else:
            p.op(eng, lambda: EN(eng).tensor_copy(out=out, in_=in_), r, w)

    def red(out, in_, r, w):
        p.op("dve", lambda: nc.vector.tensor_reduce(out=out, in_=in_, axis=AX.X, op=ALU.add), r, w)

    def recip(out, in_, r, w):
        p.op("dve", lambda: nc.vector.reciprocal(out=out, in_=in_), r, w)

    def memset(ap, val, w, eng="dve"):
        p.op(eng, lambda: EN(eng).memset(ap, val), [], w)

    def ld(out, in_, w, eng="sp", join=False, r=(), slow=False):
        h = nc.sync if eng == "sp" else (nc.gpsimd if eng == "pool" else nc.scalar)
        if slow:
            p.dma(eng, lambda: h.dma_start(out=out, in_=in_, allow_slow_non_contiguous=True), reads=list(r),
                  writes=[w], sem=w, join=join)
        else:
            p.dma(eng, lambda: h.dma_start(out=out, in_=in_), reads=list(r), writes=[w], sem=w, join=join)

    def st(out, in_, rbuf, eng="sp"):
        h = nc.sync if eng == "sp" else nc.gpsimd
        p.dma(eng, lambda: h.dma_start(out=out, in_=in_), reads=[rbuf], writes=[], sem=rbuf)

    def dump(name, ap_sb, rbufs):
        if name not in dbg_out:
            return
        b = p.buf("dbg_" + name)
        p.dma("pool", lambda: nc.gpsimd.dma_start(out=dbg_out[name], in_=ap_sb), reads=list(rbufs), writes=[],
              sem=b)

    B_ident = p.buf("ident")
    B_const = p.buf("const")
    B_c = p.buf("c")
    B_pos = p.buf("pos")
    B_cs = p.buf("cs")
    B_trig = p.buf("trig")
    B_pp = p.buf("pp")
    B_A1 = p.buf("A1")
    B_A2 = p.buf("A2")
    B_lam = p.buf("lam")
    B_g1 = p.buf("g1bc")
    B_g2 = p.buf("g2bc")
    B_wring = [p.buf(f"wring{i}") for i in range(4)]
    B_xs = [p.buf(f"xs{i}") for i in range(3)]
    B_xn = None
    B_hT = [p.buf(f"hT{t}") for t in range(NT)]
    B_qT = [p.buf(f"qT{t}") for t in range(NT)]
    B_kT = [p.buf(f"kT{t}") for t in range(NT)]
    B_vA = [p.buf(f"vA{t}") for t in range(NT)]
    B_rqT = [p.buf(f"rqT{t}") for t in range(NT)]
    B_rkT = [p.buf(f"rkT{t}") for t in range(NT)]
    B_rktok = [p.buf(f"rktok{t}") for t in range(NT)]
    B_rv = [p.buf(f"rv{t}") for t in range(NT)]
    B_rgs = [p.buf(f"rgs{g}") for g in range(4)]
    B_tmpA = None
    B_tmpB = None
    B_ss8 = [p.buf(f"ss8{i}") for i in range(2)]
    B_qtok = [p.buf(f"qtok{i}") for i in range(2)]
    B_rsl3 = [p.buf(f"rsl{i}") for i in range(3)]

    ld(c_sb, c_d, B_c)
    ld(pos_i, pos_d, B_pos)
    for t_ in range(3):
        ld(xs[t_], x_d[t_ * 128:(t_ + 1) * 128, :], B_xs[t_])
    B_st = [p.buf(f"nstat{t}") for t in range(NT)]
    memset(ssq, 0.0, B_st, eng="pool")
    memset(ones_bf, 1.0, [B_const])
    memset(onesf, 1.0, [B_const])
    memset(epsb, EPS, [B_const])
    memset(halfpi, math.pi / 2.0, [B_const])
    ld(ident, ident_d, B_ident, eng="pool")
    ld(maskb, maskb_d, B_ident, eng="pool", join=True)
    ld(mask01, mask01_d, B_ident, eng="pool", join=True)
    ld(invf, invf_d.partition_broadcast(128), B_const, join=True)
    B_decg = p.buf("decg")
    ld(decg, dec_d, B_decg)
    ld(qg_bc, qg_d.partition_broadcast(128), B_const, join=True)
    ld(kg_bc, kg_d.partition_broadcast(128), B_const, join=True)
    ld(subg, subg_d.rearrange("o p -> p o"), B_const, join=True)
    for i in range(4):
        ld(lamrow[0:1, i * 64:(i + 1) * 64], lam_d[i], B_const, join=True)

    act(cs_f, c_sb, AF.Silu, [B_c], [B_cs])
    cp(cs_bf, cs_f, [B_cs], [B_cs])
    cp(cs_bc, cs_f.unsqueeze(2).to_broadcast([128, KC, 128]), [B_cs], [B_cs])

    cp(pos_f, pos_i, [B_pos], [B_trig])
    for t in range(NT):
        ts(angT[:, t, :], invf[:, 0:40], pos_f[:, t:t + 1], None, ALU.mult, None, [B_const, B_trig], [B_trig])
        stt(angT[:, t, :], invf[:, 40:80], pos_f[:, t:t + 1], angT[:, t, :], ALU.mult, ALU.add,
            [B_const, B_trig], [B_trig])
    angT2 = angT.rearrange("p a b -> p (a b)")
    angN2 = angN.rearrange("p a b -> p (a b)")
    angI2 = angI.rearrange("p a b -> p (a b)")
    sin2 = sinT.rearrange("p a b -> p (a b)")
    cos2 = cosT.rearrange("p a b -> p (a b)")
    for which in (0, 1):
        ts(angN2, angT2, 1.0 / TWO_PI, 0.25 * which, ALU.mult, ALU.add, [B_trig], [B_trig])
        cp(angI2, angN2, [B_trig], [B_trig])
        cp(angN2, angI2, [B_trig], [B_trig])
        dst = cos2 if which else sin2
        stt(dst, angN2, -CW1, angT2, ALU.mult, ALU.add, [B_trig], [B_trig])
        stt(dst, angN2, -CW2, dst, ALU.mult, ALU.add, [B_trig], [B_trig])
        if which:
            ts(dst, dst, math.pi / 2.0, None, ALU.add, None, [B_trig], [B_trig])
        ts(dst, dst, 3.1415925, -3.1415925, ALU.min, ALU.max, [B_trig], [B_trig])
        act(dst, dst, AF.Sin, [B_trig], [B_trig])

    tt(lamrow[0:1, 0:64], lamrow[0:1, 0:64], lamrow[0:1, 64:128], ALU.mult, [B_const], [B_lam])
    tt(lamrow[0:1, 128:192], lamrow[0:1, 128:192], lamrow[0:1, 192:256], ALU.mult, [B_const, B_lam], [B_lam])
    red(lamtmp[0:1, 0:1], lamrow[0:1, 0:64], [B_lam], [B_lam])
    red(lamtmp[0:1, 1:2], lamrow[0:1, 128:192], [B_lam], [B_lam])
    act(lamtmp[0:1, 2:4], lamtmp[0:1, 0:2], AF.Exp, [B_lam], [B_lam])
    tt(lamtmp[0:1, 4:5], lamtmp[0:1, 3:4], lamtmp[0:1, 2:3], ALU.subtract, [B_lam], [B_lam])
    ts(lamtmp[0:1, 5:6], lamtmp[0:1, 4:5], -LAMBDA_INIT, None, ALU.add, None, [B_lam], [B_lam])
    mm(psum[7][:, 0:1], onesf[0:1, 0:128], lamtmp[0:1, 5:6], True, True, [B_lam, B_const], [PS[7]])
    cp(nlam, psum[7][:, 0:1], [PS[7]], [B_lam])
    ts(sg08, subg, 1.0 - LAMBDA_INIT, None, ALU.mult, None, [B_const], [B_lam])
    CD = [float(np.exp(128.0 * np.log(1.0 - 2.0 ** (-5.0 - h)))) for h in range(4)]

    B_tmpA = [p.buf(f"tmpA{i}", after=[B_trig]) for i in range(2)]
    B_tmpB = [p.buf(f"tmpB{i}", after=[B_trig]) for i in range(2)]

    ring_state = [0]

    RING_ORDER = ["a0", "a1", "a2", "a3", "w0", "a4", "w1", "a5", "w2", "a6", "w3", "a7", "w4", "a8", "w5", "a9",
                  "a10", "a11"]
    ring_issued = [0]

    def ring_src(tag):
        n_ = int(tag[1:])
        if tag[0] == "a":
            return adaw_d[:, n_ * 512:(n_ + 1) * 512]
        return win_d[:, n_ * 512:(n_ + 1) * 512]

    def ring_slot(j):
        return j if j < 4 else (j - 4) % 3

    def ring_load(tag):
        k = ring_state[0]
        ring_state[0] += 1
        assert RING_ORDER[k] == tag, (k, tag)
        while ring_issued[0] < len(RING_ORDER):
            j = ring_issued[0]
            prev = -1 if j < 4 else (j - 4 if j < 7 else j - 3)
            if prev >= k:
                break
            ring_issued[0] += 1
            i = ring_slot(j)
            v = ring_src(RING_ORDER[j]).rearrange("(k p) e -> p k e", p=128)
            ld(wring[i][:, 0:4, :], v[:, 0:4, :], B_wring[i], eng="pool")
            ld(wring[i][:, 4:8, :], v[:, 4:8, :], B_wring[i], eng="pool", join=True)
        return ring_slot(k)

    PPV = {0: 0, 1: 1, 3: 2, 4: 3}
    ppps = psum[6]

    def ada_block(blk, bi):
        vec = blk // 2
        half = blk % 2
        slot = ring_load(f"a{blk}")
        bi = 2 + bi
        ps = psum[bi]
        if vec in PPV:
            for cc in range(4):
                col = PPV[vec] * 8 + half * 4 + cc
                for kc in range(KC):
                    mm(ppps[:, col:col + 1], wring[slot][:, kc, cc * 128:(cc + 1) * 128], cs_bf[:, kc:kc + 1],
                       kc == 0, kc == KC - 1, [B_cs, B_wring[slot]], [PS[6]])
        else:
            for kc in range(KC):
                mm(ps, cs_bc[:, kc, :], wring[slot][:, kc, :], kc == 0, kc == KC - 1, [B_cs, B_wring[slot]], [PS[bi]])
            hs = slice(half * 512, (half + 1) * 512)
            if vec == 2:
                stt(g1bc[:, hs], ps, 0.5, g1bc[:, hs], ALU.mult, ALU.add, [PS[bi], B_g1], [B_g1])
            else:
                stt(g2bc[:, hs], ps, 1.0, g2bc[:, hs], ALU.mult, ALU.add, [PS[bi], B_g2], [B_g2])

    ld(pp[:, 32:40], n1g_d.rearrange("o (j p) -> p (o j)", p=128), B_pp, slow=True)
    ld(pp[:, 40:48], n2g_d.rearrange("o (j p) -> p (o j)", p=128), B_pp, join=True, slow=True)
    for j6 in range(6):
        ld(adab_pp[:, j6 * 8:(j6 + 1) * 8], adab_d[0:1, j6 * 1024:(j6 + 1) * 1024].rearrange("o (j p) -> p (o j)", p=128),
           B_pp, join=True, slow=True)
    def load_gate_bias():
        ld(g1bc, adab_d[0:1, 2048:3072].partition_broadcast(128), B_g1)
        ld(g2bc, adab_d[0:1, 5120:6144].partition_broadcast(128), B_g2)
        ts(g1bc, g1bc, 0.5, None, ALU.mult, None, [B_g1], [B_g1])

    for blk in range(4):
        ada_block(blk, blk % 2)
    tt(pp[:, 0:8], ppps[:, 0:8], adab_pp[:, 0:8], ALU.add, [PS[6], B_pp], [B_pp])
    tt(pp[:, 8:16], ppps[:, 8:16], adab_pp[:, 8:16], ALU.add, [PS[6], B_pp], [B_pp])
    stt(A1, pp[:, 8:16], 1.0, pp[:, 32:40], ALU.add, ALU.mult, [B_pp], [B_A1])
    SH1 = pp[:, 0:8]

    def pipeline(n, stages):
        ns = len(stages)
        for step in range(n + ns - 1):
            for si in range(ns - 1, -1, -1):
                t = step - si
                if 0 <= t < n:
                    stages[si](t)

    ACT_KC = (0, 4)

    def norm_stages(x_of, xbuf_of, dstT, dstbufs, Asc, Ash, Abufs, xnl, B_xnl, junk, B_junk, pre=None, bankf=None):
        def n0(t):
            if pre is not None:
                pre(t)
            act(junk, x_of(t), AF.Square, [xbuf_of(t)], [B_junk, B_st[t]], accum=ssq[:, t:t + 1])

        def n0b(t):
            act(lntmp[:, t:t + 1], ssq[:, t:t + 1], AF.Ln, [B_st[t]], [B_st[t]], scale=1.0 / D, bias=epsb[:, 0:1])
            act(rstd1[:, t:t + 1], lntmp[:, t:t + 1], AF.Exp, [B_st[t]], [B_st[t]], scale=-0.5)

        def n1(t):
            i = t % 2
            ts(xnl[i], x_of(t), rstd1[:, t:t + 1], None, ALU.mult, None, [xbuf_of(t), B_st[t]], [B_xnl[i]])
            for kc in range(KC):
                bk = bankf(i, kc) if bankf else ((2 + i) if kc in ACT_KC else (4 + i))
                pv = psum_bf[bk].rearrange("p (k a) -> p k a", a=128)
                tr(pv[:, kc, :], xnl[i][:, kc * 128:(kc + 1) * 128], [B_xnl[i]], [PS[bk]])

        def n2(t):
            i = t % 2
            for kc in range(KC):
                o = dstT[:, kc, t * 128:(t + 1) * 128]
                bk = bankf(i, kc) if bankf else ((2 + i) if kc in ACT_KC else (4 + i))
                pv = psum_bf[bk].rearrange("p (k a) -> p k a", a=128)
                if kc in ACT_KC:
                    act(o, pv[:, kc, :], AF.Identity, [PS[bk]] + Abufs, [dstbufs[t]], scale=Asc[:, kc:kc + 1],
                        bias=Ash[:, kc:kc + 1], join=kc > 0)
                else:
                    ts(o, pv[:, kc, :], Asc[:, kc:kc + 1], Ash[:, kc:kc + 1], ALU.mult, ALU.add,
                       [PS[bk]] + Abufs, [dstbufs[t]], join=kc > 0)
        return [n0, n0b, n1, n2]

    if stop_after == "p0":
        return fin()
    def ldx(t):
        if t >= 3:
            ld(xs[t % 3], x_d[t * 128:(t + 1) * 128, :], B_xs[t % 3])
        if t == 8:
            load_gate_bias()
    xn1 = [V(66 * KB + i * 2 * KB, [D]) for i in range(2)]
    B_xn = [p.buf(f"xn{i}") for i in range(2)]
    junkA = V(64 * KB, [D])
    B_junkA = p.buf("junkA")
    norm1_stages = norm_stages(lambda t: xs[t % 3], lambda t: B_xs[t % 3], hT, B_hT, A1, SH1, [B_A1, B_pp], xn1, B_xn,
                               junkA, B_junkA, pre=ldx, bankf=lambda i, kc: 2 if kc in ACT_KC else 3)

    B_rv = [p.buf(f"rv{t}", after=[B_wring[3]]) for t in range(NT)]
    if stop_after == "p1":
        return fin()
    cosA = cosT[:, :, 0:8]
    sinA = sinT[:, :, 0:8]
    CC = xnB[0][:, 0:512].bitcast(F32).rearrange("p (a b) -> p a b", b=16)
    SS = xnB[0][:, 512:1024].bitcast(F32).rearrange("p (a b) -> p a b", b=16)
    B_cs2 = p.buf("cs2", after=[B_trig])
    cp(CC[:, :, 0:8], cosA, [B_trig], [B_cs2])
    cp(CC[:, :, 8:16], cosA, [B_trig], [B_cs2])
    cp(SS[:, :, 0:8], sinA, [B_trig], [B_cs2])
    cp(SS[:, :, 8:16], sinA, [B_trig], [B_cs2])
    cosR = cosT[:, :, 8:40]
    sinR = sinT[:, :, 8:40]

    r_bufs = []
    PJ = [0, 1, 7]

    pj_off = [0]

    def proj_stage(slot):
        pjo = pj_off[0]

        def f(t):
            bi = PJ[(t + pjo) % 3]
            for kc in range(KC):
                mm(psum[bi], hT[:, kc, t * 128:(t + 1) * 128], wring[slot][:, kc, :], kc == 0, kc == KC - 1,
                   [B_hT[t], B_wring[slot]], [PS[bi]])
        return f

    tbq = [V(172 * KB + i * 2 * KB, [512], F32) for i in range(2)]
    sqq = [V(176 * KB + i * KB, [512]) for i in range(2)]
    rpq = V(178 * KB, [256], F32)
    B_tbq = [p.buf(f"tbq{i}", after=[B_trig, B_junkA] + B_xn) for i in range(2)]
    B_sqq = [p.buf(f"sqq{i}", after=[B_trig] + B_xn) for i in range(2)]
    B_rpq = p.buf("rpq", after=[B_trig] + B_xn)
    ss8q = [ss8[0], ss8[1], rs8[0]]
    B_ss8q = [B_ss8[0], B_ss8[1], p.buf("ss8c")]

    def qk_stages(slot, gbc, dstT, dstbufs, rsl, B_rsl):
        pjo = pj_off[0]
        def a1(t):
            ps, PSb = psum[PJ[(t + pjo) % 3]], PS[PJ[(t + pjo) % 3]]
            act(sqq[t % 2], ps, AF.Square, [PSb], [B_sqq[t % 2]])

        def a2(t):
            i = t % 2
            ps, PSb = psum[PJ[(t + pjo) % 3]], PS[PJ[(t + pjo) % 3]]
            red(ss8q[t % 3], sqq[i].rearrange("p (a b) -> p a b", b=64), [B_sqq[i]], [B_ss8q[t % 3]])
            tt(tbq[i].rearrange("p (a b) -> p a b", b=64), ps.rearrange("p (a b) -> p a b", b=64),
               gbc.unsqueeze(1).to_broadcast([128, 8, 64]), ALU.mult, [PSb, B_const], [B_tbq[i]])

        def a3(t):
            i = t % 2
            act(rsl[t % 3], ss8q[t % 3], AF.Ln, [B_ss8q[t % 3]], [B_rsl[t % 3]], scale=1.0 / 64, bias=epsb[:, 0:1])
            act(rsl[t % 3], rsl[t % 3], AF.Exp, [B_rsl[t % 3]], [B_rsl[t % 3]], scale=-0.5)
            tb = tbq[i].rearrange("p (a b) -> p a b", b=64)
            x16 = tb[:, :, 0:16]
            x1_ = tb[:, :, 0:8]
            x2_ = tb[:, :, 8:16]
            cb = CC[:, t, :].unsqueeze(1).to_broadcast([128, 8, 16])
            sb = SS[:, t, :].unsqueeze(1).to_broadcast([128, 8, 16])
            pa = rpq[:, 0:128].rearrange("p (a b) -> p a b", b=16)
            pb = rpq[:, 128:256].rearrange("p (a b) -> p a b", b=16)
            tt(pa, x16, cb, ALU.mult, [B_tbq[i], B_cs2], [B_rpq], eng="pool")
            tt(pb, x16, sb, ALU.mult, [B_tbq[i], B_cs2], [B_rpq], eng="pool")
            tt(x1_, pa[:, :, 0:8], pb[:, :, 8:16], ALU.subtract, [B_rpq], [B_tbq[i]], eng="pool")
            tt(x2_, pa[:, :, 8:16], pb[:, :, 0:8], ALU.add, [B_rpq], [B_tbq[i]], eng="pool")

        def a4(t):
            i = t % 2
            tt(qtok[i].rearrange("p (a b) -> p a b", b=64), tbq[i].rearrange("p (a b) -> p a b", b=64),
               rsl[t % 3].unsqueeze(2).to_broadcast([128, 8, 64]), ALU.mult, [B_tbq[i], B_rsl[t % 3]], [B_qtok[i]])

        def a5(t):
            i = t % 2
            pq = psum_bf[4 + i][:, 0:512].rearrange("p (a b) -> p a b", b=128)
            for pr in range(4):
                tr(pq[:, pr, :], qtok[i][:, pr * 128:(pr + 1) * 128], [B_qtok[i]], [PS[4 + i]])

        def a6(t):
            i = t % 2
            pq = psum_bf[4 + i][:, 0:512].rearrange("p (a b) -> p a b", b=128)
            cp(dstT[:, :, t * 128:(t + 1) * 128], pq, [PS[4 + i]], [dstbufs[t]], eng="act")
        return [proj_stage(slot), a1, a2, a3, a4, a5, a6]

    def v_stages(slot, dst, dstbufs):
        pjo = pj_off[0]
        def s1(t):
            cp(dst[:, t, :], psum[PJ[(t + pjo) % 3]], [PS[PJ[(t + pjo) % 3]]], [dstbufs[t]], eng="act")
        return [proj_stage(slot), s1]

    def r_stages(slot):
        pjo = pj_off[0]
        B_tmpA = [p.buf(f"rtA{i}", after=B_tbq) for i in range(2)]
        B_tmpB = [p.buf(f"rtB{i}", after=B_sqq + [B_rpq, B_const, B_trig]) for i in range(2)]
        r_bufs.extend(B_tmpA + B_tmpB)
        def s1(t):
            i = t % 2
            ps, PSb = psum[PJ[(t + pjo) % 3]], PS[PJ[(t + pjo) % 3]]
            pv = ps.rearrange("p (a b) -> p a b", b=64)
            x1_ = pv[:, :, 0:32]
            x2_ = pv[:, :, 32:64]
            cb = cosR[:, t, :].unsqueeze(1).to_broadcast([128, 8, 32])
            sb = sinR[:, t, :].unsqueeze(1).to_broadcast([128, 8, 32])
            ta = tmpA[i].rearrange("p (a b) -> p a b", b=64)
            tb = tmpB[i].rearrange("p (a b) -> p a b", b=64)
            tt(ta[:, :, 0:32], x1_, cb, ALU.mult, [PSb, B_trig], [B_tmpA[i]])
            tt(ta[:, :, 32:64], x2_, cb, ALU.mult, [PSb, B_trig], [B_tmpA[i]])
            tt(tb[:, :, 0:32], x2_, sb, ALU.mult, [PSb, B_trig], [B_tmpB[i]])
            tt(tb[:, :, 32:64], x1_, sb, ALU.mult, [PSb, B_trig], [B_tmpB[i]])

        def s2(t):
            i = t % 2
            ta = tmpA[i].rearrange("p (a b) -> p a b", b=64)
            tb = tmpB[i].rearrange("p (a b) -> p a b", b=64)
            tt(ta[:, :, 0:32], ta[:, :, 0:32], tb[:, :, 0:32], ALU.subtract, [B_tmpA[i], B_tmpB[i]], [B_tmpA[i]],
               eng="pool")
            tt(ta[:, :, 32:64], ta[:, :, 32:64], tb[:, :, 32:64], ALU.add, [B_tmpA[i], B_tmpB[i]], [B_tmpA[i]],
               eng="pool")
            qt = qtok[i][:, 0:256].rearrange("p (a b) -> p a b", b=64)
            for h_ in range(4):
                act(qt[:, h_, :], ta[:, h_, :], AF.Copy, [B_tmpA[i], B_decg], [B_qtok[i]],
                    scale=decg[:, t * 8 + h_:t * 8 + h_ + 1], join=h_ > 0)
            kt_ = rktok[:, t, :].rearrange("p (a b) -> p a b", b=64)
            tt(kt_, ta[:, 4:8, :], decg[:, t * 8 + 4:t * 8 + 8].unsqueeze(2).to_broadcast([128, 4, 64]), ALU.mult,
               [B_tmpA[i], B_decg], [B_rktok[t]], eng="pool")

        def s3(t):
            i = t % 2
            pq = psum_bf[4 + i][:, 0:512].rearrange("p (a b) -> p a b", b=128)
            for pr in range(2):
                tr(pq[:, pr, :], qtok[i][:, pr * 128:(pr + 1) * 128], [B_qtok[i]], [PS[4 + i]])
            for pr in range(2):
                tr(pq[:, 2 + pr, :], rktok[:, t, pr * 128:(pr + 1) * 128], [B_rktok[t]], [PS[4 + i]])
            cp(rqT[:, :, t * 128:(t + 1) * 128], pq[:, 0:2, :], [PS[4 + i]], [B_rqT[t]], eng="act")
            cp(rkT[:, :, t * 128:(t + 1) * 128], pq[:, 2:4, :], [PS[4 + i]], [B_rkT[t]], eng="act")
        return [proj_stage(slot), s1, s2, s3]

    blk_stages = {}

    NPRE = 4

    def blk_hook(step):
        if step == 0:
            kb = 0
        elif step >= NT + NPRE and (step - NPRE) % NT == 0 and (step - NPRE) // NT < 5:
            kb = (step - NPRE) // NT
        else:
            return
        if kb > 0:
            ada_block(3 + kb, (3 + kb) % 2)
        sl = ring_load(f"w{kb}")
        pj_off[0] = kb % 3
        nop4 = [lambda t: None] * len(norm1_stages)
        if kb == 0:
            blk_stages[kb] = norm1_stages + qk_stages(sl, qg_bc, qT, B_qT, rsl3, B_rsl3)
        elif kb == 1:
            blk_stages[kb] = nop4 + qk_stages(sl, kg_bc, kT, B_kT, rsl3, B_rsl3)
        elif kb == 2:
            B_vA[:] = [p.buf(f"vA{t}", after=[B_junkA] + B_xn) for t in range(NT)]
            blk_stages[kb] = nop4 + v_stages(sl, vA, B_vA)
        elif kb == 3:
            blk_stages[kb] = nop4 + r_stages(sl)
        else:
            blk_stages[kb] = nop4 + v_stages(sl, rv, B_rv)

    MAXS = 11
    for step in range(5 * NT + MAXS - 1):
        blk_hook(step)
        for si in range(MAXS - 1, -1, -1):
            gi = step - si
            if 0 <= gi < 5 * NT:
                st_ = blk_stages.get(gi // NT)
                if st_ is None:
                    assert si < NPRE
                    continue
                if si < len(st_):
                    st_[si](gi % NT)
    ada_block(8, 0)
    dump("hT", hT, B_hT)
    slot = ring_load("w5")
    n = 0
    for fc in range(4):
        for g in range(4):
            bi = PJ[n % 3]
            n += 1
            for kc in range(KC):
                mm(psum[bi], wring[slot][:, kc, fc * 128:(fc + 1) * 128], hT[:, kc, g * 512:(g + 1) * 512],
                   kc == 0, kc == KC - 1, [B_wring[slot]] + B_hT[4 * g:4 * g + 4], [PS[bi]])
            act(rgsT[:, fc, g * 512:(g + 1) * 512], psum[bi], AF.Silu, [PS[bi]], [B_rgs[g]])
    ada_block(9, 1)
    ada_block(10, 0)
    ada_block(11, 1)
    dump("qT", qT, B_qT)
    dump("kT", kT, B_kT)
    dump("vA", vA, B_vA)
    dump("rqT", rqT, B_rqT)
    dump("rkT", rkT, B_rkT)
    dump("rktok", rktok, B_rktok)
    dump("rv", rv, B_rv)
    dump("rgsT", rgsT, B_rgs)

    tt(pp[:, 16:24], ppps[:, 16:24], adab_pp[:, 24:32], ALU.add, [PS[6], B_pp], [B_pp])
    tt(pp[:, 24:32], ppps[:, 24:32], adab_pp[:, 32:40], ALU.add, [PS[6], B_pp], [B_pp])
    stt(A2, pp[:, 24:32], 1.0, pp[:, 40:48], ALU.add, ALU.mult, [B_pp], [B_A2])
    SH2 = pp[:, 16:24]

    if stop_after == "p2":
        return fin()
    B_tmpA = r_bufs[0:2] + B_tbq
    B_tmpB = r_bufs[2:4] + B_sqq + [B_rpq]
    B_oaT = [p.buf(f"oaT{g}", after=B_wring) for g in range(4)]
    NPT = 4
    ptmo = [misc_end_phaseA]
    PT = [V(172 * KB + i * KB, [512]) for i in range(NPT)]
    B_PT = [p.buf(f"PT{i}", after=B_tmpA + B_tmpB) for i in range(NPT)]
    ofp = [V(176 * KB + i * 2 * KB, [512], F32) for i in range(2)]
    B_of = [p.buf(f"of{i}", after=B_tmpA + B_tmpB) for i in range(2)]
    r0t = V(136 * KB, [512], F32)
    r1t = V(138 * KB, [512], F32)
    t1t = V(140 * KB, [512], F32)
    sqb = V(142 * KB, [512])
    rst = V(144 * KB, [512], F32)
    B_at = p.buf("attn_tmp", after=B_xs + B_xn)

    B_r0 = p.buf("attn_r0", after=[B_at])
    B_r1 = p.buf("attn_r1", after=[B_at])
    B_sq = p.buf("attn_sq", after=[B_at])
    SB = [0, 1, 7]
    NB = len(SB)
    its = []
    for u, (h, g) in enumerate([(h, g) for h in range(4) for g in range(4)]):
        nkt = 4 * g + 4
        for s_ in range(2):
            for kt in range(nkt):
                its.append((u, h, g, s_, kt, nkt))

    def geom(i):
        u, h, g, s_, kt, nkt = its[i]
        jj = max(0, kt - 4 * g)
        return u, h, g, s_, kt, nkt, jj * 128, kt >= 4 * g

    B_kz = [[p.buf(f"kz{a}{b}") for b in range(3)] for a in range(2)]
    for a in range(2):
        for b in range(3):
            zr = slice(64, 128) if a == 0 else slice(0, 64)
            memset(kzr[a][b][zr, :], 0.0, [B_kz[a][b]], eng="pool")

    def KZ(i):
        u, h, g, s_, kt, nkt, c0, diag = geom(i)
        prt = slice(s_ * 64, (s_ + 1) * 64)
        p.op("pool", lambda: nc.gpsimd.tensor_copy(out=kzr[s_][i % 3][prt, :], in_=kT[prt, h, kt * 128:(kt + 1) * 128]),
             [B_kT[kt]], [B_kz[s_][i % 3]])

    def ST(i):
        u, h, g, s_, kt, nkt, c0, diag = geom(i)
        sbi = SB[i % NB]
        mm(psum[sbi][:, c0:512], kzr[s_][i % 3], qT[:, h, g * 512 + c0:(g + 1) * 512],
           True, not diag, [B_kz[s_][i % 3]] + B_qT[4 * g + c0 // 128:4 * g + 4], [PS[sbi]])
        if diag:
            mm(psum[sbi][:, c0:c0 + 128], ident, maskb, False, True, [B_ident, B_const], [PS[sbi]])

    def EXP(i):
        u, h, g, s_, kt, nkt, c0, diag = geom(i)
        sbi = SB[i % NB]
        act(PT[i % NPT][:, c0:512], psum[sbi][:, c0:512], AF.Exp, [PS[sbi]], [B_PT[i % NPT]], scale=0.125)

    def PV(i):
        u, h, g, s_, kt, nkt, c0, diag = geom(i)
        pti = i % NPT
        mm(psum[2 + s_][:, c0:512], vA[:, kt, h * 128:(h + 1) * 128], PT[pti][:, c0:512],
           kt == 0, kt == nkt - 1, [B_vA[kt], B_PT[pti]], [PS[2 + s_]])
        mm(psum[4 + s_][:, c0:512], ones_bf, PT[pti][:, c0:512],
           kt == 0, kt == nkt - 1, [B_const, B_PT[pti]], [PS[4 + s_]])

    def E1a(u):
        recip(r0t, psum[4], [PS[4]], [B_r0])
        tt(t1t, psum[2], r0t, ALU.mult, [PS[2], B_r0], [B_r0])

    def E1b(u):
        o = ofp[u % 2]
        recip(r1t, psum[5], [PS[5]], [B_r1])
        stt(o, psum[3], nlam[:, 0:1], r1t, ALU.mult, ALU.mult, [PS[3], B_r1, B_lam], [B_of[u % 2]])
        tt(o, o, t1t, ALU.add, [B_of[u % 2], B_r0], [B_of[u % 2]])

    B_rs = p.buf("attn_rst", after=[B_at])

    def E2a(u):
        o = ofp[u % 2]
        tt(sqb, o, o, ALU.mult, [B_of[u % 2]], [B_sq], eng="pool")

    def E2b(u):
        mm(psum[6], ones_bf, sqb, True, True, [B_const, B_sq], [PS[6]])

    def E2c(u):
        act(rst, psum[6], AF.Ln, [PS[6]], [B_rs], scale=1.0 / 128, bias=epsb[:, 0:1])

    def E2d(u):
        act(rst, rst, AF.Exp, [B_rs], [B_rs], scale=-0.5)

    def E2e(u):
        h, g = u // 4, u % 4
        o = ofp[u % 2]
        stt(oaT[:, h, g * 512:(g + 1) * 512], o, sg08[:, 0:1], rst, ALU.mult, ALU.mult,
            [B_of[u % 2], B_rs, B_lam], [B_oaT[g]])
    E2_STEPS = [(8, E2a), (12, E2b), (14, E2c), (16, E2d), (18, E2e)]

    nit = len(its)
    e2_at = {}
    KZ(0)
    KZ(1)
    for i in range(nit + NB):
        if i + 2 < nit:
            KZ(i + 2)
        j = i - NB
        if j >= 0:
            EXP(j)
            PV(j)
            u, h, g, s_, kt, nkt = its[j]
            if kt == nkt - 1:
                if s_ == 0:
                    E1a(u)
                    if u == 15:
                        memset(psum[2], 0.0, [PS[2]])
                else:
                    E1b(u)
                    for off, fn_ in E2_STEPS:
                        e2_at.setdefault(j + off, []).append((fn_, u))
            for fn_, uu in e2_at.pop(j, []):
                fn_(uu)
        if i < nit:
            ST(i)
    for j in sorted(e2_at):
        for fn_, uu in e2_at[j]:
            fn_(uu)
    B_at = p.buf("attn_tmp_all", after=[B_r0, B_r1, B_sq, B_rs])
    dump("oaT", oaT, B_oaT)

    if stop_after == "p3":
        return fin()
    B_obT = [p.buf(f"obT{t}", after=B_qT) for t in range(NT)]
    scm = [V(172 * KB + i * KB, [512]) for i in range(2)]
    B_scm = [p.buf(f"scm{i}", after=B_PT) for i in range(2)]
    Uf = V(136 * KB, [4, 128], F32)
    Rf = V(138 * KB, [4, 128], F32)
    Rbf2 = [V(140 * KB + i * 512, [2, 128]) for i in range(2)]
    Rbf2 = [V(140 * KB + i * KB, [4, 128]) for i in range(2)]
    sqr = [V(142 * KB + i * KB, [512]) for i in range(2)]
    rsr = [V(144 * KB + i * 2 * KB, [512], F32) for i in range(2)]
    otr = [V(176 * KB + i * 2 * KB, [512], F32) for i in range(2)]
    B_U = p.buf("U", after=[B_at])
    B_R = [p.buf(f"Rbf{i}", after=[B_at] + B_xn) for i in range(2)]
    B_rt = [p.buf(f"ret_tmp{i}", after=[B_at] + B_of + B_xn) for i in range(2)]

    cdt = V(164 * KB, [4, 128], F32)
    B_cdt = p.buf("cdt", after=B_wring)

    def blk(h):
        return (h % 2) * 2 + h // 2
    OB = [3, 4, 5]
    SSB = [6, 7]

    def rt0(n_):
        tk = slice(n_ * 128, (n_ + 1) * 128)
        for h in (0, 2, 1, 3):
            prt = slice((h % 2) * 64, (h % 2) * 64 + 64)
            bk = h % 2
            mm(psum[bk][:, blk(h) * 128:(blk(h) + 1) * 128], rkT[prt, h // 2, tk], rqT[prt, h // 2, tk], True, True,
               [B_rkT[n_], B_rqT[n_]], [PS[bk]])
        for h in range(4):
            pr = h // 2
            p.op("pe", (lambda h=h, pr=pr: nc.tensor.matmul(
                psum[2][:, h * 128:(h + 1) * 128], lhsT=rktok[:, n_, pr * 128:(pr + 1) * 128],
                rhs=rv[:, n_, h * 128:(h + 1) * 128], start=False, stop=False, skip_group_check=True)),
                [B_rktok[n_], B_rv[n_]], [PS[2]])

    def rt1(n_):
        i = n_ % 2
        tt(scm[i][:, 0:256].rearrange("p (a b) -> p a b", b=128), psum[0][:, 0:256].rearrange("p (a b) -> p a b", b=128),
           mask01.unsqueeze(1).to_broadcast([128, 2, 128]), ALU.mult, [PS[0], B_const, B_ident], [B_scm[i]])
        tt(scm[i][:, 256:512].rearrange("p (a b) -> p a b", b=128),
           psum[1][:, 256:512].rearrange("p (a b) -> p a b", b=128),
           mask01.unsqueeze(1).to_broadcast([128, 2, 128]), ALU.mult, [PS[1], B_const, B_ident, B_scm[i]], [B_scm[i]])
        if n_ < NT - 1:
            cp(Rbf2[(n_ + 1) % 2].rearrange("p a b -> p (a b)"), psum[2], [PS[2]], [B_R[(n_ + 1) % 2]], eng="act")

    def rt2(n_):
        i = n_ % 2
        tk = slice(n_ * 128, (n_ + 1) * 128)
        b_o = OB[n_ % 3]
        for h in range(4):
            prt = slice((h % 2) * 64, (h % 2) * 64 + 64)
            mm(psum[b_o][:, h * 128:(h + 1) * 128], rv[:, n_, h * 128:(h + 1) * 128],
               scm[i][:, blk(h) * 128:(blk(h) + 1) * 128], True, n_ == 0, [B_rv[n_], B_scm[i]], [PS[b_o]])
            if n_ > 0:
                mm(psum[b_o][:, h * 128:(h + 1) * 128], Rbf2[i][prt, h, :], rqT[prt, h // 2, tk], False, True,
                   [B_R[i], B_rqT[n_]], [PS[b_o]])

    otr4 = otr + [V(164 * KB + i * 2 * KB, [512], F32) for i in range(2)]
    B_o4 = [p.buf(f"ret_o{i}", after=[B_at] + B_of + B_wring) for i in range(4)]
    B_sqr = [p.buf(f"ret_sq{i}", after=[B_at] + B_xn) for i in range(2)]

    def rt3(n_):
        i = n_ % 2
        b_o = OB[n_ % 3]
        act(sqr[i], psum[b_o], AF.Square, [PS[b_o]], [B_sqr[i]])
        cp(otr4[n_ % 4], psum[b_o], [PS[b_o]], [B_o4[n_ % 4]])

    def rt3b(n_):
        i = n_ % 2
        mm(psum[SSB[i]], ones_bf, sqr[i], True, True, [B_const, B_sqr[i]], [PS[SSB[i]]])

    def rt3c(n_):
        i = n_ % 2
        act(rsr[i], psum[SSB[i]], AF.Ln, [PS[SSB[i]]], [B_rt[i]], scale=1.0 / 128, bias=epsb[:, 0:1])
        act(rsr[i], rsr[i], AF.Exp, [B_rt[i]], [B_rt[i]], scale=-0.5)

    def rt4(n_):
        i = n_ % 2
        tk = slice(n_ * 128, (n_ + 1) * 128)
        o_ = otr4[n_ % 4]
        tt(o_, o_, rsr[i], ALU.mult, [B_o4[n_ % 4], B_rt[i]], [B_o4[n_ % 4]])
        tt(obT[:, :, tk], o_.rearrange("p (a b) -> p a b", b=128), rgsT[:, :, tk], ALU.mult,
           [B_o4[n_ % 4], B_rgs[n_ // 4]], [B_obT[n_]])

    pipeline(NT, [rt0, rt1, rt2, rt3, rt3b, rt3c, rt4])
    dump("obT", obT, B_obT)

    if stop_after == "p4":
        return fin()
    B_wga = p.buf("wga", after=B_kT)
    B_wgb = p.buf("wgb", after=B_vA)
    B_wa = p.buf("wa", after=B_rv[0:8])
    B_wb = p.buf("wb", after=B_rv[8:16])
    B_yT = [p.buf(f"yT{g}", after=B_rqT + B_rkT + B_rktok + B_rgs) for g in range(4)]
    for kc2 in range(2):
        v = win_d[:, 3072:4096].rearrange("(k p) e -> p k e", p=128)
        ld(wga[:, kc2 * 4:(kc2 + 1) * 4, :], v[:, kc2 * 4:(kc2 + 1) * 4, :], B_wga, eng="pool", join=kc2 > 0)
    for kc2 in range(2):
        v = win_d[:, 4096:5120].rearrange("(k p) e -> p k e", p=128)
        ld(wgb[:, kc2 * 4:(kc2 + 1) * 4, :], v[:, kc2 * 4:(kc2 + 1) * 4, :], B_wgb, eng="pool", join=kc2 > 0)
    ld(wab[:, 0:4, :], wa_d.rearrange("(k p) e -> p k e", p=128), B_wa, eng="pool")
    ld(wab[:, 4:8, :], wb_d.rearrange("(k p) e -> p k e", p=128), B_wb, eng="pool")
    tha = [V(136 * KB + i * 2 * KB, [512], F32) for i in range(2)]
    thb = [V(140 * KB + i * 2 * KB, [512], F32) for i in range(2)]
    pra = [V(112 * KB + i * 2 * KB, [512], F32) for i in range(2)]
    prb = [V(116 * KB + i * 2 * KB, [512], F32) for i in range(2)]
    B_yt = [p.buf(f"y_tmp{i}", after=[B_U] + B_R + B_rt + B_o4 + B_sqr + B_scm + B_rgs) for i in range(2)]
    n = 0
    for cch in range(KC):
        cs_ = slice(cch * 128, (cch + 1) * 128)
        for g in range(4):
            i = n % 2
            n += 1
            tg = slice(g * 512, (g + 1) * 512)
            b_ga, b_gb, b_ya, b_yb = (0, 1, 2, 3) if i == 0 else (4, 5, 6, 7)
            for kc in range(KC):
                mm(psum[b_ga], wga[:, kc, cs_], hT[:, kc, tg], kc == 0, kc == KC - 1,
                   [B_wga] + B_hT[4 * g:4 * g + 4], [PS[b_ga]])
            for kc in range(KC):
                mm(psum[b_gb], wgb[:, kc, cs_], hT[:, kc, tg], kc == 0, kc == KC - 1,
                   [B_wgb] + B_hT[4 * g:4 * g + 4], [PS[b_gb]])
            for hh in range(4):
                mm(psum[b_ya], wab[:, hh, cs_], oaT[:, hh, tg], hh == 0, hh == 3, [B_wa, B_oaT[g]], [PS[b_ya]])
            for hh in range(4):
                mm(psum[b_yb], wab[:, 4 + hh, cs_], obT[:, hh, tg], hh == 0, hh == 3,
                   [B_wb] + B_obT[4 * g:4 * g + 4], [PS[b_yb]])
            act(tha[i], psum[b_ga], AF.Tanh, [PS[b_ga]], [B_yt[i]], scale=0.5)
            act(thb[i], psum[b_gb], AF.Tanh, [PS[b_gb]], [B_yt[i]], scale=0.5)
            stt(pra[i], tha[i], 1.0, psum[b_ya], ALU.add, ALU.mult, [B_yt[i], PS[b_ya]], [B_yt[i]])
            stt(prb[i], thb[i], 1.0, psum[b_yb], ALU.add, ALU.mult, [B_yt[i], PS[b_yb]], [B_yt[i]])
            tt(yT[:, cch, tg], pra[i], prb[i], ALU.add, [B_yt[i]], [B_yT[g]])
    dump("yT", yT, B_yT)

    if stop_after == "p5":
        return fin()
    B_wout = p.buf("wout", after=B_wring + B_scm + B_rt + B_o4 + B_sqr + B_PT + B_of + B_tmpA + B_tmpB + [B_cdt])
    B_x1 = [p.buf(f"x1_{t}", after=B_hT + B_obT + [B_wga]) for t in range(NT)]
    B_h2T = [p.buf(f"h2T{t}", after=B_rgs + [B_wa, B_wb] + B_xs + B_xn + B_yt) for t in range(NT)]
    v = wout_d.rearrange("(k p) e -> p k e", p=128)
    ld(woutS[:, 0:4, :], v[:, 0:4, :], B_wout, eng="pool")
    ld(woutS[:, 4:8, :], v[:, 4:8, :], B_wout, eng="pool", join=True)
    for kc in range(KC):
        tt(woutS[:, kc, :], woutS[:, kc, :], g1bc, ALU.mult, [B_wout, B_g1], [B_wout], eng="pool")
    for t in range(NT):
        ld(x1[:, t, :], x_d[t * 128:(t + 1) * 128, :], B_x1[t])
    memset(ssq, 0.0, B_st)
    WB = [(0, 1), (6, 7)]

    def w0(t):
        tk = slice(t * 128, (t + 1) * 128)
        for ch in range(2):
            bi = WB[t % 2][ch]
            for kc in range(KC):
                mm(psum[bi], yT[:, kc, tk], woutS[:, kc, ch * 512:(ch + 1) * 512], kc == 0, kc == KC - 1,
                   [B_yT[t // 4], B_wout], [PS[bi]])

    def w1(t):
        for ch in range(2):
            bi = WB[t % 2][ch]
            tt(x1[:, t, ch * 512:(ch + 1) * 512], x1[:, t, ch * 512:(ch + 1) * 512], psum[bi], ALU.add,
               [B_x1[t], PS[bi]], [B_x1[t]])
    B_xnB = [p.buf(f"xnB{i}", after=[B_cs2, B_decg]) for i in range(2)]
    junkB = V(sgt_off, [D])
    B_junkB = p.buf("junkB", after=[B_trig])
    pipeline(NT, [w0, w1] + norm_stages(lambda t: x1[:, t, :], lambda t: B_x1[t], h2T, B_h2T, A2, SH2,
                                        [B_A2, B_pp], xnB, B_xnB, junkB, B_junkB))
    dump("x1", x1, B_x1)
    dump("h2T", h2T, B_h2T)

    if stop_after == "p6":
        return fin()
    GROUPS = [(0, 6), (6, 6), (12, 5), (17, 5)]
    B_act = [p.buf(f"act{i}", after=B_yT + [B_wgb]) for i in range(2)]
    B_wgu = [p.buf(f"wgu{i}", after=B_wring + B_oaT + B_xn + [B_at] + B_rt + B_R + B_sqr) for i in range(3)]
    B_wdn = [p.buf(f"wdn{i}", after=B_wring + B_oaT + B_PT + B_of + B_scm + B_rt + B_o4 + B_sqr + B_yt + B_tmpA + B_tmpB + [B_wout])
             for i in range(2)]
    sgt = [V(sgt_off + i * 2 * KB, [512], F32) for i in range(2)]
    B_sg = [p.buf(f"sg{i}", after=[B_trig, B_junkB]) for i in range(2)]
    wgu_n = [0]

    def gate_up(gi):
        j0, nj = GROUPS[gi]
        a = actS[gi % 2]
        n2 = 0
        for jj in range(nj):
            j = j0 + jj
            si = wgu_n[0] % 3
            wgu_n[0] += 1
            vg = wgu_d[:, j * 128:(j + 1) * 128].rearrange("(k p) e -> p k e", p=128)
            vu = wgu_d[:, DFF + j * 128:DFF + (j + 1) * 128].rearrange("(k p) e -> p k e", p=128)
            ld(wguS[si][:, :, 0:128], vg, B_wgu[si], eng="pool")
            ld(wguS[si][:, :, 128:256], vu, B_wgu[si], eng="pool", join=True)
            for g in range(4):
                i = n2 % 2
                n2 += 1
                tg = slice(g * 512, (g + 1) * 512)
                b_g, b_u = (4, 5) if i == 0 else (6, 7)
                for kc in range(KC):
                    mm(psum[b_g], wguS[si][:, kc, 0:128], h2T[:, kc, tg], kc == 0, kc == KC - 1,
                       [B_wgu[si]] + B_h2T[4 * g:4 * g + 4], [PS[b_g]])
                for kc in range(KC):
                    mm(psum[b_u], wguS[si][:, kc, 128:256], h2T[:, kc, tg], kc == 0, kc == KC - 1,
                       [B_wgu[si]] + B_h2T[4 * g:4 * g + 4], [PS[b_u]])
                act(sgt[i], psum[b_g], AF.Silu, [PS[b_g]], [B_sg[i]])
                tt(a[:, jj, tg], sgt[i], psum[b_u], ALU.mult, [B_sg[i], PS[b_u]], [B_act[gi % 2]])

    def down(gi):
        j0, nj = GROUPS[gi]
        a = actS[gi % 2]
        wi = gi % 2
        vd = wdn_d[j0 * 128:(j0 + nj) * 128, :].rearrange("(k p) e -> p k e", p=128)
        ld(wdnS[wi][:, 0:nj, :], vd, B_wdn[wi], eng="pool")
        for jj in range(nj):
            tt(wdnS[wi][:, jj, :], wdnS[wi][:, jj, :], g2bc, ALU.mult, [B_wdn[wi], B_g2], [B_wdn[wi]], eng="pool")
        for t in range(NT):
            tk = slice(t * 128, (t + 1) * 128)
            for ch in range(2):
                bi = (2 * t + ch) % 4
                for jj in range(nj):
                    mm(psum[bi], a[:, jj, tk], wdnS[wi][:, jj, ch * 512:(ch + 1) * 512], jj == 0, jj == nj - 1,
                       [B_act[gi % 2], B_wdn[wi]], [PS[bi]])
                tt(x1[:, t, ch * 512:(ch + 1) * 512], x1[:, t, ch * 512:(ch + 1) * 512], psum[bi], ALU.add,
                   [B_x1[t], PS[bi]], [B_x1[t]])
            if gi == len(GROUPS) - 1:
                st(out_d[tk, :], x1[:, t, :], B_x1[t])

    gate_up(0)
    gate_up(1)
    down(0)
    gate_up(2)
    down(1)
    gate_up(3)
    down(2)
    down(3)

    return fin()


def _consts():
    f32 = np.float32
    half_a = 8
    inv_a = (f32(500000.0) ** (-(np.arange(half_a, dtype=f32)) / f32(half_a))).astype(f32)
    half_r = 32
    inv_r = (f32(10000.0) ** (-(np.arange(half_r, dtype=f32)) / f32(half_r))).astype(f32)
    ia64 = 500000.0 ** (-(np.arange(half_a, dtype=np.float64)) / half_a)
    ir64 = 10000.0 ** (-(np.arange(half_r, dtype=np.float64)) / half_r)
    i64 = np.concatenate([ia64, ir64])
    hi = i64.astype(f32)
    lo = (i64 - hi.astype(np.float64)).astype(f32)
    invf = np.concatenate([hi, lo])[None, :].astype(f32)
    lg = np.log(1.0 - 2.0 ** (-5.0 - np.arange(4, dtype=np.float64)))
    tpos = (np.arange(NT, dtype=np.float64)[None, :, None] * 128.0 + np.arange(128, dtype=np.float64)[:, None, None] + 1.0)
    qdec = np.exp(lg[None, None, :] * tpos)
    kdec = np.exp(-lg[None, None, :] * tpos) * (64.0 ** -0.5)
    dec = np.concatenate([qdec, kdec], axis=2).reshape(128, NT * 8).astype(f32)
    ident = np.eye(128, dtype=f32)
    k = np.arange(128)[:, None]
    q = np.arange(128)[None, :]
    mask01 = (q >= k).astype(f32)
    maskb = np.where(q >= k, 0.0, NEG).astype(f32)
    return dict(k_invf=invf, k_dec=dec, k_ident=ident, k_maskb=maskb, k_mask01=mask01)


def make_in_maps(inputs):
    x = np.asarray(inputs["x"], dtype=np.float32)
    c = np.asarray(inputs["c"], dtype=np.float32)
    pos = np.asarray(inputs["positions"], dtype=np.int32)
    shared = {}
    for k in ("ada_w", "w_in", "w_branch_a", "w_branch_b", "w_out", "w_gate_up", "w_down"):
        shared[k] = np.ascontiguousarray(np.asarray(inputs[k], dtype=np.float32)[0])
    for k in ("ada_b", "norm1_g", "q_norm_g", "k_norm_g", "lambda_q1", "lambda_k1", "lambda_q2", "lambda_k2",
              "subln_g", "norm2_g"):
        shared[k] = np.ascontiguousarray(np.asarray(inputs[k], dtype=np.float32)[0][None, :])
    shared.update(_consts())
    maps = []
    for b in range(8):
        m = dict(shared)
        m["x"] = np.ascontiguousarray(x[b])
        m["c"] = np.ascontiguousarray(c[b].reshape(KC, 128).T)
        m["pos"] = np.ascontiguousarray(pos[b].reshape(NT, 128).T)
        maps.append(m)
    return maps


def kernel(**inputs):
    nc, _ = build_nc()
    maps = make_in_maps(inputs)
    res = run_bass_kernel_spmd(nc, maps, core_ids=list(range(8)))
    out = np.stack([np.asarray(res.results[b]["out"], dtype=np.float32).reshape(S, D) for b in range(8)], axis=0)
    return out
```

```python
import math
import numpy as np
import concourse.bass as bass
import concourse.mybir as mybir
from concourse.bass_utils import run_bass_kernel_spmd

F32 = mybir.dt.float32
BF16 = mybir.dt.bfloat16
I32 = mybir.dt.int32
AF = mybir.ActivationFunctionType
ALU = mybir.AluOpType
AX = mybir.AxisListType

S = 2048
D = 1024
NT = 16
KC = 8
DFF = 2816
NFC = 22
EPS = 1e-6
LAMBDA_INIT = 0.8 - 0.6 * math.exp(-0.3 * 0)
TWO_PI = 2.0 * math.pi
CW1 = 6.28125
CW2 = TWO_PI - CW1
NEG = -30000.0


class Buf:
    __slots__ = ("name", "w", "rs", "sem", "cnt", "psum")

    def __init__(self, name):
        self.name = name
        self.w = []
        self.rs = []
        self.sem = None
        self.cnt = 0
        self.psum = False


class Prog:
    ENGS = ("pe", "act", "dve", "pool", "sp")

    def __init__(self, nc):
        self.nc = nc
        self.ins = []

    def buf(self, name, after=()):
        b = Buf(name)
        for o in after:
            b.w = b.w + o.w
            b.rs = b.rs + o.rs
        return b

    def _rec(self, eng, fn, reads, writes, join, dma, dsem):
        idx = len(self.ins)
        deps = {}
        for b in reads:
            for i in b.w:
                deps[i] = "RAW"
            if b.psum:
                for i in b.rs:
                    if self.ins[i]["eng"] != eng and i not in deps:
                        deps[i] = "RAR"
        for b in writes:
            if not join:
                for i in b.w:
                    deps[i] = "WAW"
            for i in b.rs:
                if i not in deps:
                    deps[i] = "WAR"
        self.ins.append(dict(eng=eng, fn=fn, deps=deps, dma=dma, dsem=dsem, inc=False, val=None))
        for b in reads:
            if not dma:
                b.rs = [i for i in b.rs if self.ins[i]["dma"] or self.ins[i]["eng"] != eng]
            b.rs.append(idx)
        for b in writes:
            if join:
                b.w = b.w + [idx]
            else:
                b.w = [idx]
            b.rs = []
        return idx

    def op(self, eng, fn, reads=(), writes=(), join=False):
        return self._rec(eng, fn, reads, writes, join, False, None)

    def dma(self, eng, fn, reads=(), writes=(), sem=None, join=False):
        return self._rec(eng, fn, reads, writes, join, True, sem)

    def emit(self, block, final_wait_eng="sp"):
        nc = self.nc
        ins = self.ins
        for rec in ins:
            for i, kind in rec["deps"].items():
                pr = ins[i]
                if pr["dma"]:
                    continue
                if pr["eng"] == rec["eng"] and rec["eng"] == "pe":
                    continue
                pr["inc"] = True
        esem = {e: nc.alloc_semaphore("prog_" + e) for e in self.ENGS}
        cnt = {e: 0 for e in self.ENGS}
        for rec in ins:
            if rec["dma"]:
                b = rec["dsem"]
                if b.sem is None:
                    b.sem = nc.alloc_semaphore("dma_" + b.name)
                b.cnt += 16
                rec["val"] = (b.sem, b.cnt)
            elif rec["inc"]:
                cnt[rec["eng"]] += 1
                rec["val"] = (esem[rec["eng"]], cnt[rec["eng"]])
        last_dma = {}
        for rec in ins:
            if rec["dma"]:
                s, v = rec["val"]
                last_dma[s.num] = (s, v)
        handles = {"pe": nc.tensor, "act": nc.scalar, "dve": nc.vector, "pool": nc.gpsimd, "sp": nc.sync}
        self.nwaits = 0

        def run(eng):
            h = handles[eng]
            seen = {}
            for rec in ins:
                if rec["eng"] != eng:
                    continue
                need = {}
                for i, kind in rec["deps"].items():
                    pr = ins[i]
                    if (not pr["dma"]) and pr["eng"] == eng and eng == "pe":
                        continue
                    s, v = pr["val"]
                    if seen.get(s.num, 0) >= v:
                        continue
                    if s.num not in need or need[s.num][1] < v:
                        need[s.num] = (s, v)
                waits = list(need.values())
                for s, v in waits:
                    seen[s.num] = v
                self.nwaits += len(waits)
                for s, v in waits[:-1]:
                    h.wait_ge(s, v)
                bi = rec["fn"]()
                if waits:
                    s, v = waits[-1]
                    bi._wait_ge(s, v)
                if rec["dma"]:
                    s, v = rec["val"]
                    bi.then_inc(s, 16)
                elif rec["inc"]:
                    s, v = rec["val"]
                    bi.then_inc(s, 1)
            if eng == final_wait_eng:
                for s, v in last_dma.values():
                    if seen.get(s.num, 0) < v:
                        h.wait_ge(s, v)

        @block.tensor
        def _(e):
            run("pe")

        @block.scalar
        def _(e):
            run("act")

        @block.vector
        def _(e):
            run("dve")

        @block.gpsimd
        def _(e):
            run("pool")

        @block.sync
        def _(e):
            run("sp")


def build_nc(debug=None, stop_after=None):
    debug = debug or {}
    nc = bass.Bass("TRN2", target_bir_lowering=False)
    p = Prog(nc)

    def din(name, shape, dt=F32):
        return nc.dram_tensor(name, list(shape), dt, kind="ExternalInput").ap()

    x_d = din("x", [S, D])
    c_d = din("c", [128, KC])
    pos_d = din("pos", [128, NT], I32)
    adaw_d = din("ada_w", [D, 6 * D])
    adab_d = din("ada_b", [1, 6 * D])
    n1g_d = din("norm1_g", [1, D])
    win_d = din("w_in", [D, 5120])
    qg_d = din("q_norm_g", [1, 64])
    kg_d = din("k_norm_g", [1, 64])
    lam_d = [din(n, [1, 64]) for n in ("lambda_q1", "lambda_k1", "lambda_q2", "lambda_k2")]
    subg_d = din("subln_g", [1, 128])
    wa_d = din("w_branch_a", [512, D])
    wb_d = din("w_branch_b", [512, D])
    wout_d = din("w_out", [D, D])
    n2g_d = din("norm2_g", [1, D])
    wgu_d = din("w_gate_up", [D, 2 * DFF])
    wdn_d = din("w_down", [DFF, D])
    invf_d = din("k_invf", [1, 80])
    dec_d = din("k_dec", [128, NT * 8])
    ident_d = din("k_ident", [128, 128])
    maskb_d = din("k_maskb", [128, 128])
    mask01_d = din("k_mask01", [128, 128])
    out_d = nc.dram_tensor("out", [S, D], F32, kind="ExternalOutput").ap()
    dbg_out = {}
    for name, shape in debug.items():
        dbg_out[name] = nc.dram_tensor("dbg_" + name, list(shape), F32, kind="ExternalOutput").ap()

    def fin():
        with nc.Block() as block:
            p.emit(block)
        return nc, p

    ARENA_KB = 207
    arena = nc.alloc_sbuf_tensor("arena", [128, ARENA_KB * 512], BF16).ap()

    def V(off_b, shape, dt=BF16, parts=128):
        esz = 4 if dt in (F32, I32) else 2
        n = 1
        for s_ in shape:
            n *= s_
        nbytes = n * esz
        assert off_b % 4 == 0 and off_b + nbytes <= ARENA_KB * 1024, (off_b, nbytes)
        a = arena[0:parts, off_b // 2:(off_b + nbytes) // 2]
        if dt != BF16:
            a = a.bitcast(dt)
        if len(shape) == 2:
            a = a.rearrange("p (a b) -> p a b", b=shape[1])
        elif len(shape) == 3:
            a = a.rearrange("p (a b c) -> p a b c", b=shape[1], c=shape[2])
        return a

    KB = 1024
    hT = V(0, [KC, S])
    qT = V(32 * KB, [4, S])
    kT = V(48 * KB, [4, S])
    vA = V(64 * KB, [NT, 512])
    rqT = V(80 * KB, [2, S])
    rkT = V(88 * KB, [2, S])
    rktok = V(96 * KB, [NT, 256])
    rgsT = V(104 * KB, [4, S])
    rv = V(120 * KB, [NT, 512])
    xs = [V(136 * KB + i * 4 * KB, [D], F32) for i in range(3)]
    xn = [V(174 * KB + i * 2 * KB, [D]) for i in range(2)]
    wring = [V(148 * KB + i * 8 * KB, [KC, 512]) for i in range(3)]
    wring.append(V(120 * KB, [KC, 512]))
    oaT = V(148 * KB, [4, S])
    obT = V(32 * KB, [4, S])
    wga = V(48 * KB, [KC, D])
    wgb = V(64 * KB, [KC, D])
    yT = V(80 * KB, [KC, S])
    wab = V(120 * KB, [8, D])
    x1 = V(0, [NT, D], F32)
    woutS = V(164 * KB, [KC, D])
    actS = [V(64 * KB + i * 24 * KB, [6, S]) for i in range(2)]
    h2T = V(112 * KB, [KC, S])
    wguS = [V(144 * KB + i * 4 * KB, [KC, 256]) for i in range(3)]
    wdnS = [V(156 * KB + i * 12 * KB, [6, D]) for i in range(2)]
    tmpA = [V(172 * KB + i * 2 * KB, [512], F32) for i in range(2)]
    tmpB = [V(176 * KB + i * 2 * KB, [512], F32) for i in range(2)]
    mo = [180 * KB]

    def M(shape, dt=BF16, parts=128):
        esz = 4 if dt in (F32, I32) else 2
        n = 1
        for s_ in shape:
            n *= s_
        off = mo[0]
        mo[0] += (n * esz + 31) // 32 * 32
        assert mo[0] <= ARENA_KB * KB, mo[0]
        return V(off, shape, dt, parts)

    ident = M([128])
    maskb = M([128])
    mask01 = M([128])
    ones_bf = M([128])
    onesf = M([128], F32)
    epsb = M([1], F32)
    halfpi = M([1], F32)
    c_sb = M([KC], F32)
    cs_f = M([KC], F32)
    cs_bf = M([KC])
    cs_bc = M([KC, 128])
    pos_i = M([NT], I32)
    pos_f = M([NT], F32)
    invf = V(172 * KB + 7680, [80], F32)
    qg_bc = M([64], F32)
    kg_bc = M([64], F32)
    subg = M([1], F32)
    sg08 = M([1], F32)
    lamrow = M([256], F32, parts=1)
    lamtmp = M([8], F32, parts=1)
    nlam = M([1], F32)
    sgt_off = mo[0]
    cosT = M([NT, 40], F32)
    sinT = M([NT, 40], F32)
    angT = V(172 * KB, [NT, 40], F32)
    angN = V(172 * KB + 2560, [NT, 40], F32)
    angI = V(172 * KB + 5120, [NT, 40], I32)
    pp = M([48], F32)
    A1 = M([KC], F32)
    A2 = M([KC], F32)
    adab_pp = M([48], F32)
    ssq = M([NT], F32)
    rstd1 = M([NT], F32)
    lntmp = M([NT], F32)
    ss8 = [M([8], F32) for _ in range(2)]
    rs8 = [M([8], F32) for _ in range(2)]
    qtok = [M([512]) for _ in range(2)]
    g1bc = M([D], F32)
    g2bc = M([D], F32)
    rsl3 = [M([8], F32) for _ in range(3)]
    kzr = [[M([128]) for _ in range(3)] for _ in range(2)]
    xnB = [M([D]) for _ in range(2)]
    decg = xnB[1][:, 0:NT * 16].bitcast(F32)
    assert mo[0] <= ARENA_KB * KB
    misc_end_phaseA = mo[0]

    psum = [nc.alloc_psum_tensor(f"ps{i}", [128, 512], F32).ap() for i in range(8)]
    psum_bf = [q.bitcast(BF16) for q in psum]
    PS = [p.buf(f"ps{i}") for i in range(8)]
    for b_ in PS:
        b_.psum = True

    def mm(out, lhsT, rhs, start, stop, r, w):
        p.op("pe", lambda: nc.tensor.matmul(out, lhsT=lhsT, rhs=rhs, start=start, stop=stop), r, w)

    def tr(out, in_, r, w):
        p.op("pe", lambda: nc.tensor.transpose(out, in_, ident), r + [B_ident], w)

    def act(out, in_, func, r, w, scale=1.0, bias=None, accum=None, join=False):
        def f():
            kw = {}
            if bias is not None:
                kw["bias"] = bias
            if accum is not None:
                kw["accum_out"] = accum
            return nc.scalar.activation(out=out, in_=in_, func=func, scale=scale, **kw)
        p.op("act", f, r, w, join=join)

    def EN(eng):
        return nc.vector if eng == "dve" else nc.gpsimd

    def tt(out, in0, in1, op, r, w, eng="dve"):
        p.op(eng, lambda: EN(eng).tensor_tensor(out=out, in0=in0, in1=in1, op=op), r, w)

    def ts(out, in0, s1, s2, op0, op1, r, w, eng="dve", join=False):
        if s2 is None:
            p.op(eng, lambda: EN(eng).tensor_scalar(out=out, in0=in0, scalar1=s1, scalar2=None, op0=op0), r, w,
                 join=join)
        else:
            p.op(eng, lambda: EN(eng).tensor_scalar(out=out, in0=in0, scalar1=s1, scalar2=s2, op0=op0, op1=op1), r, w,
                 join=join)

    def stt(out, in0, scalar, in1, op0, op1, r, w, eng="dve"):
        p.op(eng, lambda: EN(eng).scalar_tensor_tensor(out=out, in0=in0, scalar=scalar, in1=in1, op0=op0, op1=op1), r, w)

    def cp(out, in_, r, w, eng="dve"):
        if eng == "act":
            p.op("act", lambda: nc.scalar.copy(out=out, in_=in_), r, w)
        else:
            p.op(eng, lambda: EN(eng).tensor_copy(out=out, in_=in_), r, w)

    def red(out, in_, r, w):
        p.op("dve", lambda: nc.vector.tensor_reduce(out=out, in_=in_, axis=AX.X, op=ALU.add), r, w)

    def recip(out, in_, r, w):
        p.op("dve", lambda: nc.vector.reciprocal(out=out, in_=in_), r, w)

    def memset(ap, val, w, eng="dve"):
        p.op(eng, lambda: EN(eng).memset(ap, val), [], w)

    def ld(out, in_, w, eng="sp", join=False, r=(), slow=False):
        h = nc.sync if eng == "sp" else (nc.gpsimd if eng == "pool" else nc.scalar)
        if slow:
            p.dma(eng, lambda: h.dma_start(out=out, in_=in_, allow_slow_non_contiguous=True), reads=list(r),
                  writes=[w], sem=w, join=join)
        else:
            p.dma(eng, lambda: h.dma_start(out=out, in_=in_), reads=list(r), writes=[w], sem=w, join=join)

    def st(out, in_, rbuf, eng="sp"):
        h = nc.sync if eng == "sp" else nc.gpsimd
        p.dma(eng, lambda: h.dma_start(out=out, in_=in_), reads=[rbuf], writes=[], sem=rbuf)

    def dump(name, ap_sb, rbufs):
        if name not in dbg_out:
            return
        b = p.buf("dbg_" + name)
        p.dma("pool", lambda: nc.gpsimd.dma_start(out=dbg_out[name], in_=ap_sb), reads=list(rbufs), writes=[],
              sem=b)

    B_ident = p.buf("ident")
    B_const = p.buf("const")
    B_c = p.buf("c")
    B_pos = p.buf("pos")
    B_cs = p.buf("cs")
    B_trig = p.buf("trig")
    B_pp = p.buf("pp")
    B_A1 = p.buf("A1")
    B_A2 = p.buf("A2")
    B_lam = p.buf("lam")
    B_g1 = p.buf("g1bc")
    B_g2 = p.buf("g2bc")
    B_wring = [p.buf(f"wring{i}") for i in range(4)]
    B_xs = [p.buf(f"xs{i}") for i in range(3)]
    B_xn = None
    B_hT = [p.buf(f"hT{t}") for t in range(NT)]
    B_qT = [p.buf(f"qT{t}") for t in range(NT)]
    B_kT = [p.buf(f"kT{t}") for t in range(NT)]
    B_vA = [p.buf(f"vA{t}") for t in range(NT)]
    B_rqT = [p.buf(f"rqT{t}") for t in range(NT)]
    B_rkT = [p.buf(f"rkT{t}") for t in range(NT)]
    B_rktok = [p.buf(f"rktok{t}") for t in range(NT)]
    B_rv = [p.buf(f"rv{t}") for t in range(NT)]
    B_rgs = [p.buf(f"rgs{g}") for g in range(4)]
    B_tmpA = None
    B_tmpB = None
    B_ss8 = [p.buf(f"ss8{i}") for i in range(2)]
    B_qtok = [p.buf(f"qtok{i}") for i in range(2)]
    B_rsl3 = [p.buf(f"rsl{i}") for i in range(3)]

    ld(c_sb, c_d, B_c)
    ld(pos_i, pos_d, B_pos)
    for t_ in range(3):
        ld(xs[t_], x_d[t_ * 128:(t_ + 1) * 128, :], B_xs[t_])
    B_st = [p.buf(f"nstat{t}") for t in range(NT)]
    memset(ssq, 0.0, B_st, eng="pool")
    memset(ones_bf, 1.0, [B_const])
    memset(onesf, 1.0, [B_const])
    memset(epsb, EPS, [B_const])
    memset(halfpi, math.pi / 2.0, [B_const])
    ld(ident, ident_d, B_ident, eng="pool")
    ld(maskb, maskb_d, B_ident, eng="pool", join=True)
    ld(mask01, mask01_d, B_ident, eng="pool", join=True)
    ld(invf, invf_d.partition_broadcast(128), B_const, join=True)
    B_decg = p.buf("decg")
    ld(decg, dec_d, B_decg)
    ld(qg_bc, qg_d.partition_broadcast(128), B_const, join=True)
    ld(kg_bc, kg_d.partition_broadcast(128), B_const, join=True)
    ld(subg, subg_d.rearrange("o p -> p o"), B_const, join=True)
    for i in range(4):
        ld(lamrow[0:1, i * 64:(i + 1) * 64], lam_d[i], B_const, join=True)

    act(cs_f, c_sb, AF.Silu, [B_c], [B_cs])
    cp(cs_bf, cs_f, [B_cs], [B_cs])
    cp(cs_bc, cs_f.unsqueeze(2).to_broadcast([128, KC, 128]), [B_cs], [B_cs])

    cp(pos_f, pos_i, [B_pos], [B_trig])
    for t in range(NT):
        ts(angT[:, t, :], invf[:, 0:40], pos_f[:, t:t + 1], None, ALU.mult, None, [B_const, B_trig], [B_trig])
        stt(angT[:, t, :], invf[:, 40:80], pos_f[:, t:t + 1], angT[:, t, :], ALU.mult, ALU.add,
            [B_const, B_trig], [B_trig])
    angT2 = angT.rearrange("p a b -> p (a b)")
    angN2 = angN.rearrange("p a b -> p (a b)")
    angI2 = angI.rearrange("p a b -> p (a b)")
    sin2 = sinT.rearrange("p a b -> p (a b)")
    cos2 = cosT.rearrange("p a b -> p (a b)")
    for which in (0, 1):
        ts(angN2, angT2, 1.0 / TWO_PI, 0.25 * which, ALU.mult, ALU.add, [B_trig], [B_trig])
        cp(angI2, angN2, [B_trig], [B_trig])
        cp(angN2, angI2, [B_trig], [B_trig])
        dst = cos2 if which else sin2
        stt(dst, angN2, -CW1, angT2, ALU.mult, ALU.add, [B_trig], [B_trig])
        stt(dst, angN2, -CW2, dst, ALU.mult, ALU.add, [B_trig], [B_trig])
        if which:
            ts(dst, dst, math.pi / 2.0, None, ALU.add, None, [B_trig], [B_trig])
        ts(dst, dst, 3.1415925, -3.1415925, ALU.min, ALU.max, [B_trig], [B_trig])
        act(dst, dst, AF.Sin, [B_trig], [B_trig])

    tt(lamrow[0:1, 0:64], lamrow[0:1, 0:64], lamrow[0:1, 64:128], ALU.mult, [B_const], [B_lam])
    tt(lamrow[0:1, 128:192], lamrow[0:1, 128:192], lamrow[0:1, 192:256], ALU.mult, [B_const, B_lam], [B_lam])
    red(lamtmp[0:1, 0:1], lamrow[0:1, 0:64], [B_lam], [B_lam])
    red(lamtmp[0:1, 1:2], lamrow[0:1, 128:192], [B_lam], [B_lam])
    act(lamtmp[0:1, 2:4], lamtmp[0:1, 0:2], AF.Exp, [B_lam], [B_lam])
    tt(lamtmp[0:1, 4:5], lamtmp[0:1, 3:4], lamtmp[0:1, 2:3], ALU.subtract, [B_lam], [B_lam])
    ts(lamtmp[0:1, 5:6], lamtmp[0:1, 4:5], -LAMBDA_INIT, None, ALU.add, None, [B_lam], [B_lam])
    mm(psum[7][:, 0:1], onesf[0:1, 0:128], lamtmp[0:1, 5:6], True, True, [B_lam, B_const], [PS[7]])
    cp(nlam, psum[7][:, 0:1], [PS[7]], [B_lam])
    ts(sg08, subg, 1.0 - LAMBDA_INIT, None, ALU.mult, None, [B_const], [B_lam])
    CD = [float(np.exp(128.0 * np.log(1.0 - 2.0 ** (-5.0 - h)))) for h in range(4)]

    B_tmpA = [p.buf(f"tmpA{i}", after=[B_trig]) for i in range(2)]
    B_tmpB = [p.buf(f"tmpB{i}", after=[B_trig]) for i in range(2)]

    ring_state = [0]

    RING_ORDER = ["a0", "a1", "a2", "a3", "w0", "a4", "w1", "a5", "w2", "a6", "w3", "a7", "w4", "a8", "w5", "a9",
                  "a10", "a11"]
    ring_issued = [0]

    def ring_src(tag):
        n_ = int(tag[1:])
        if tag[0] == "a":
            return adaw_d[:, n_ * 512:(n_ + 1) * 512]
        return win_d[:, n_ * 512:(n_ + 1) * 512]

    def ring_slot(j):
        return j if j < 4 else (j - 4) % 3

    def ring_load(tag):
        k = ring_state[0]
        ring_state[0] += 1
        assert RING_ORDER[k] == tag, (k, tag)
        while ring_issued[0] < len(RING_ORDER):
            j = ring_issued[0]
            prev = -1 if j < 4 else (j - 4 if j < 7 else j - 3)
            if prev >= k:
                break
            ring_issued[0] += 1
            i = ring_slot(j)
            v = ring_src(RING_ORDER[j]).rearrange("(k p) e -> p k e", p=128)
            ld(wring[i][:, 0:4, :], v[:, 0:4, :], B_wring[i], eng="pool")
            ld(wring[i][:, 4:8, :], v[:, 4:8, :], B_wring[i], eng="pool", join=True)
        return ring_slot(k)

    PPV = {0: 0, 1: 1, 3: 2, 4: 3}
    ppps = psum[6]

    def ada_block(blk, bi):
        vec = blk // 2
        half = blk % 2
        slot = ring_load(f"a{blk}")
        bi = 2 + bi
        ps = psum[bi]
        if vec in PPV:
            for cc in range(4):
                col = PPV[vec] * 8 + half * 4 + cc
                for kc in range(KC):
                    mm(ppps[:, col:col + 1], wring[slot][:, kc, cc * 128:(cc + 1) * 128], cs_bf[:, kc:kc + 1],
                       kc == 0, kc == KC - 1, [B_cs, B_wring[slot]], [PS[6]])
        else:
            for kc in range(KC):
                mm(ps, cs_bc[:, kc, :], wring[slot][:, kc, :], kc == 0, kc == KC - 1, [B_cs, B_wring[slot]], [PS[bi]])
            hs = slice(half * 512, (half + 1) * 512)
            if vec == 2:
                stt(g1bc[:, hs], ps, 0.5, g1bc[:, hs], ALU.mult, ALU.add, [PS[bi], B_g1], [B_g1])
            else:
                stt(g2bc[:, hs], ps, 1.0, g2bc[:, hs], ALU.mult, ALU.add, [PS[bi], B_g2], [B_g2])

    ld(pp[:, 32:40], n1g_d.rearrange("o (j p) -> p (o j)", p=128), B_pp, slow=True)
    ld(pp[:, 40:48], n2g_d.rearrange("o (j p) -> p (o j)", p=128), B_pp, join=True, slow=True)
    for j6 in range(6):
        ld(adab_pp[:, j6 * 8:(j6 + 1) * 8], adab_d[0:1, j6 * 1024:(j6 + 1) * 1024].rearrange("o (j p) -> p (o j)", p=128),
           B_pp, join=True, slow=True)
    ld(g1bc, adab_d[0:1, 2048:3072].partition_broadcast(128), B_g1)
    ld(g2bc, adab_d[0:1, 5120:6144].partition_broadcast(128), B_g2)
    ts(g1bc, g1bc, 0.5, None, ALU.mult, None, [B_g1], [B_g1])

    for blk in range(4):
        ada_block(blk, blk % 2)
    tt(pp[:, 0:8], ppps[:, 0:8], adab_pp[:, 0:8], ALU.add, [PS[6], B_pp], [B_pp])
    tt(pp[:, 8:16], ppps[:, 8:16], adab_pp[:, 8:16], ALU.add, [PS[6], B_pp], [B_pp])
    stt(A1, pp[:, 8:16], 1.0, pp[:, 32:40], ALU.add, ALU.mult, [B_pp], [B_A1])
    SH1 = pp[:, 0:8]

    def pipeline(n, stages):
        ns = len(stages)
        for step in range(n + ns - 1):
            for si in range(ns - 1, -1, -1):
                t = step - si
                if 0 <= t < n:
                    stages[si](t)

    ACT_KC = (0, 4)

    def norm_stages(x_of, xbuf_of, dstT, dstbufs, Asc, Ash, Abufs, xnl, B_xnl, junk, B_junk, pre=None, bankf=None):
        def n0(t):
            if pre is not None:
                pre(t)
            act(junk, x_of(t), AF.Square, [xbuf_of(t)], [B_junk, B_st[t]], accum=ssq[:, t:t + 1])

        def n0b(t):
            act(lntmp[:, t:t + 1], ssq[:, t:t + 1], AF.Ln, [B_st[t]], [B_st[t]], scale=1.0 / D, bias=epsb[:, 0:1])
            act(rstd1[:, t:t + 1], lntmp[:, t:t + 1], AF.Exp, [B_st[t]], [B_st[t]], scale=-0.5)

        def n1(t):
            i = t % 2
            ts(xnl[i], x_of(t), rstd1[:, t:t + 1], None, ALU.mult, None, [xbuf_of(t), B_st[t]], [B_xnl[i]])
            for kc in range(KC):
                bk = bankf(i, kc) if bankf else ((2 + i) if kc in ACT_KC else (4 + i))
                pv = psum_bf[bk].rearrange("p (k a) -> p k a", a=128)
                tr(pv[:, kc, :], xnl[i][:, kc * 128:(kc + 1) * 128], [B_xnl[i]], [PS[bk]])

        def n2(t):
            i = t % 2
            for kc in range(KC):
                o = dstT[:, kc, t * 128:(t + 1) * 128]
                bk = bankf(i, kc) if bankf else ((2 + i) if kc in ACT_KC else (4 + i))
                pv = psum_bf[bk].rearrange("p (k a) -> p k a", a=128)
                if kc in ACT_KC:
                    act(o, pv[:, kc, :], AF.Identity, [PS[bk]] + Abufs, [dstbufs[t]], scale=Asc[:, kc:kc + 1],
                        bias=Ash[:, kc:kc + 1], join=kc > 0)
                else:
                    ts(o, pv[:, kc, :], Asc[:, kc:kc + 1], Ash[:, kc:kc + 1], ALU.mult, ALU.add,
                       [PS[bk]] + Abufs, [dstbufs[t]], join=kc > 0)
        return [n0, n0b, n1, n2]

    if stop_after == "p0":
        return fin()
    def ldx(t):
        if t >= 3:
            ld(xs[t % 3], x_d[t * 128:(t + 1) * 128, :], B_xs[t % 3])
    xn1 = [V(66 * KB + i * 2 * KB, [D]) for i in range(2)]
    B_xn = [p.buf(f"xn{i}") for i in range(2)]
    junkA = V(64 * KB, [D])
    B_junkA = p.buf("junkA")
    norm1_stages = norm_stages(lambda t: xs[t % 3], lambda t: B_xs[t % 3], hT, B_hT, A1, SH1, [B_A1, B_pp], xn1, B_xn,
                               junkA, B_junkA, pre=ldx, bankf=lambda i, kc: 2 if kc in ACT_KC else 3)

    B_rv = [p.buf(f"rv{t}", after=[B_wring[3]]) for t in range(NT)]
    if stop_after == "p1":
        return fin()
    cosA = cosT[:, :, 0:8]
    sinA = sinT[:, :, 0:8]
    CC = xnB[0][:, 0:512].bitcast(F32).rearrange("p (a b) -> p a b", b=16)
    SS = xnB[0][:, 512:1024].bitcast(F32).rearrange("p (a b) -> p a b", b=16)
    B_cs2 = p.buf("cs2", after=[B_trig])
    cp(CC[:, :, 0:8], cosA, [B_trig], [B_cs2])
    cp(CC[:, :, 8:16], cosA, [B_trig], [B_cs2])
    cp(SS[:, :, 0:8], sinA, [B_trig], [B_cs2])
    cp(SS[:, :, 8:16], sinA, [B_trig], [B_cs2])
    cosR = cosT[:, :, 8:40]
    sinR = sinT[:, :, 8:40]

    r_bufs = []
    PJ = [0, 1, 7]

    pj_off = [0]

    def proj_stage(slot):
        pjo = pj_off[0]

        def f(t):
            bi = PJ[(t + pjo) % 3]
            for kc in range(KC):
                mm(psum[bi], hT[:, kc, t * 128:(t + 1) * 128], wring[slot][:, kc, :], kc == 0, kc == KC - 1,
                   [B_hT[t], B_wring[slot]], [PS[bi]])
        return f

    tbq = [V(172 * KB + i * 2 * KB, [512], F32) for i in range(2)]
    sqq = [V(176 * KB + i * KB, [512]) for i in range(2)]
    rpq = V(178 * KB, [256], F32)
    B_tbq = [p.buf(f"tbq{i}", after=[B_trig, B_junkA] + B_xn) for i in range(2)]
    B_sqq = [p.buf(f"sqq{i}", after=[B_trig] + B_xn) for i in range(2)]
    B_rpq = p.buf("rpq", after=[B_trig] + B_xn)
    ss8q = [ss8[0], ss8[1], rs8[0]]
    B_ss8q = [B_ss8[0], B_ss8[1], p.buf("ss8c")]

    def qk_stages(slot, gbc, dstT, dstbufs, rsl, B_rsl):
        pjo = pj_off[0]
        def a1(t):
            ps, PSb = psum[PJ[(t + pjo) % 3]], PS[PJ[(t + pjo) % 3]]
            act(sqq[t % 2], ps, AF.Square, [PSb], [B_sqq[t % 2]])

        def a2(t):
            i = t % 2
            ps, PSb = psum[PJ[(t + pjo) % 3]], PS[PJ[(t + pjo) % 3]]
            red(ss8q[t % 3], sqq[i].rearrange("p (a b) -> p a b", b=64), [B_sqq[i]], [B_ss8q[t % 3]])
            tt(tbq[i].rearrange("p (a b) -> p a b", b=64), ps.rearrange("p (a b) -> p a b", b=64),
               gbc.unsqueeze(1).to_broadcast([128, 8, 64]), ALU.mult, [PSb, B_const], [B_tbq[i]])

        def a3(t):
            i = t % 2
            act(rsl[t % 3], ss8q[t % 3], AF.Ln, [B_ss8q[t % 3]], [B_rsl[t % 3]], scale=1.0 / 64, bias=epsb[:, 0:1])
            act(rsl[t % 3], rsl[t % 3], AF.Exp, [B_rsl[t % 3]], [B_rsl[t % 3]], scale=-0.5)
            tb = tbq[i].rearrange("p (a b) -> p a b", b=64)
            x16 = tb[:, :, 0:16]
            x1_ = tb[:, :, 0:8]
            x2_ = tb[:, :, 8:16]
            cb = CC[:, t, :].unsqueeze(1).to_broadcast([128, 8, 16])
            sb = SS[:, t, :].unsqueeze(1).to_broadcast([128, 8, 16])
            pa = rpq[:, 0:128].rearrange("p (a b) -> p a b", b=16)
            pb = rpq[:, 128:256].rearrange("p (a b) -> p a b", b=16)
            tt(pa, x16, cb, ALU.mult, [B_tbq[i], B_cs2], [B_rpq], eng="pool")
            tt(pb, x16, sb, ALU.mult, [B_tbq[i], B_cs2], [B_rpq], eng="pool")
            tt(x1_, pa[:, :, 0:8], pb[:, :, 8:16], ALU.subtract, [B_rpq], [B_tbq[i]], eng="pool")
            tt(x2_, pa[:, :, 8:16], pb[:, :, 0:8], ALU.add, [B_rpq], [B_tbq[i]], eng="pool")

        def a4(t):
            i = t % 2
            tt(qtok[i].rearrange("p (a b) -> p a b", b=64), tbq[i].rearrange("p (a b) -> p a b", b=64),
               rsl[t % 3].unsqueeze(2).to_broadcast([128, 8, 64]), ALU.mult, [B_tbq[i], B_rsl[t % 3]], [B_qtok[i]])

        def a5(t):
            i = t % 2
            pq = psum_bf[4 + i][:, 0:512].rearrange("p (a b) -> p a b", b=128)
            for pr in range(4):
                tr(pq[:, pr, :], qtok[i][:, pr * 128:(pr + 1) * 128], [B_qtok[i]], [PS[4 + i]])

        def a6(t):
            i = t % 2
            pq = psum_bf[4 + i][:, 0:512].rearrange("p (a b) -> p a b", b=128)
            cp(dstT[:, :, t * 128:(t + 1) * 128], pq, [PS[4 + i]], [dstbufs[t]], eng="act")
        return [proj_stage(slot), a1, a2, a3, a4, a5, a6]

    def v_stages(slot, dst, dstbufs):
        pjo = pj_off[0]
        def s1(t):
            cp(dst[:, t, :], psum[PJ[(t + pjo) % 3]], [PS[PJ[(t + pjo) % 3]]], [dstbufs[t]], eng="act")
        return [proj_stage(slot), s1]

    def r_stages(slot):
        pjo = pj_off[0]
        B_tmpA = [p.buf(f"rtA{i}", after=B_tbq) for i in range(2)]
        B_tmpB = [p.buf(f"rtB{i}", after=B_sqq + [B_rpq, B_const, B_trig]) for i in range(2)]
        r_bufs.extend(B_tmpA + B_tmpB)
        def s1(t):
            i = t % 2
            ps, PSb = psum[PJ[(t + pjo) % 3]], PS[PJ[(t + pjo) % 3]]
            pv = ps.rearrange("p (a b) -> p a b", b=64)
            x1_ = pv[:, :, 0:32]
            x2_ = pv[:, :, 32:64]
            cb = cosR[:, t, :].unsqueeze(1).to_broadcast([128, 8, 32])
            sb = sinR[:, t, :].unsqueeze(1).to_broadcast([128, 8, 32])
            ta = tmpA[i].rearrange("p (a b) -> p a b", b=64)
            tb = tmpB[i].rearrange("p (a b) -> p a b", b=64)
            tt(ta[:, :, 0:32], x1_, cb, ALU.mult, [PSb, B_trig], [B_tmpA[i]])
            tt(ta[:, :, 32:64], x2_, cb, ALU.mult, [PSb, B_trig], [B_tmpA[i]])
            tt(tb[:, :, 0:32], x2_, sb, ALU.mult, [PSb, B_trig], [B_tmpB[i]])
            tt(tb[:, :, 32:64], x1_, sb, ALU.mult, [PSb, B_trig], [B_tmpB[i]])

        def s2(t):
            i = t % 2
            ta = tmpA[i].rearrange("p (a b) -> p a b", b=64)
            tb = tmpB[i].rearrange("p (a b) -> p a b", b=64)
            tt(ta[:, :, 0:32], ta[:, :, 0:32], tb[:, :, 0:32], ALU.subtract, [B_tmpA[i], B_tmpB[i]], [B_tmpA[i]],
               eng="pool")
            tt(ta[:, :, 32:64], ta[:, :, 32:64], tb[:, :, 32:64], ALU.add, [B_tmpA[i], B_tmpB[i]], [B_tmpA[i]],
               eng="pool")
            qt = qtok[i][:, 0:256].rearrange("p (a b) -> p a b", b=64)
            for h_ in range(4):
                act(qt[:, h_, :], ta[:, h_, :], AF.Copy, [B_tmpA[i], B_decg], [B_qtok[i]],
                    scale=decg[:, t * 8 + h_:t * 8 + h_ + 1], join=h_ > 0)
            kt_ = rktok[:, t, :].rearrange("p (a b) -> p a b", b=64)
            tt(kt_, ta[:, 4:8, :], decg[:, t * 8 + 4:t * 8 + 8].unsqueeze(2).to_broadcast([128, 4, 64]), ALU.mult,
               [B_tmpA[i], B_decg], [B_rktok[t]], eng="pool")

        def s3(t):
            i = t % 2
            pq = psum_bf[4 + i][:, 0:512].rearrange("p (a b) -> p a b", b=128)
            for pr in range(2):
                tr(pq[:, pr, :], qtok[i][:, pr * 128:(pr + 1) * 128], [B_qtok[i]], [PS[4 + i]])
            for pr in range(2):
                tr(pq[:, 2 + pr, :], rktok[:, t, pr * 128:(pr + 1) * 128], [B_rktok[t]], [PS[4 + i]])
            cp(rqT[:, :, t * 128:(t + 1) * 128], pq[:, 0:2, :], [PS[4 + i]], [B_rqT[t]], eng="act")
            cp(rkT[:, :, t * 128:(t + 1) * 128], pq[:, 2:4, :], [PS[4 + i]], [B_rkT[t]], eng="act")
        return [proj_stage(slot), s1, s2, s3]

    blk_stages = {}

    NPRE = 4

    def blk_hook(step):
        if step == 0:
            kb = 0
        elif step >= NT + NPRE and (step - NPRE) % NT == 0 and (step - NPRE) // NT < 5:
            kb = (step - NPRE) // NT
        else:
            return
        if kb > 0:
            ada_block(3 + kb, (3 + kb) % 2)
        sl = ring_load(f"w{kb}")
        pj_off[0] = kb % 3
        nop4 = [lambda t: None] * len(norm1_stages)
        if kb == 0:
            blk_stages[kb] = norm1_stages + qk_stages(sl, qg_bc, qT, B_qT, rsl3, B_rsl3)
        elif kb == 1:
            blk_stages[kb] = nop4 + qk_stages(sl, kg_bc, kT, B_kT, rsl3, B_rsl3)
        elif kb == 2:
            B_vA[:] = [p.buf(f"vA{t}", after=[B_junkA] + B_xn) for t in range(NT)]
            blk_stages[kb] = nop4 + v_stages(sl, vA, B_vA)
        elif kb == 3:
            blk_stages[kb] = nop4 + r_stages(sl)
        else:
            blk_stages[kb] = nop4 + v_stages(sl, rv, B_rv)

    MAXS = 11
    for step in range(5 * NT + MAXS - 1):
        blk_hook(step)
        for si in range(MAXS - 1, -1, -1):
            gi = step - si
            if 0 <= gi < 5 * NT:
                st_ = blk_stages.get(gi // NT)
                if st_ is None:
                    assert si < NPRE
                    continue
                if si < len(st_):
                    st_[si](gi % NT)
    ada_block(8, 0)
    dump("hT", hT, B_hT)
    slot = ring_load("w5")
    n = 0
    for fc in range(4):
        for g in range(4):
            bi = PJ[n % 3]
            n += 1
            for kc in range(KC):
                mm(psum[bi], wring[slot][:, kc, fc * 128:(fc + 1) * 128], hT[:, kc, g * 512:(g + 1) * 512],
                   kc == 0, kc == KC - 1, [B_wring[slot]] + B_hT[4 * g:4 * g + 4], [PS[bi]])
            act(rgsT[:, fc, g * 512:(g + 1) * 512], psum[bi], AF.Silu, [PS[bi]], [B_rgs[g]])
    ada_block(9, 1)
    ada_block(10, 0)
    ada_block(11, 1)
    dump("qT", qT, B_qT)
    dump("kT", kT, B_kT)
    dump("vA", vA, B_vA)
    dump("rqT", rqT, B_rqT)
    dump("rkT", rkT, B_rkT)
    dump("rktok", rktok, B_rktok)
    dump("rv", rv, B_rv)
    dump("rgsT", rgsT, B_rgs)

    tt(pp[:, 16:24], ppps[:, 16:24], adab_pp[:, 24:32], ALU.add, [PS[6], B_pp], [B_pp])
    tt(pp[:, 24:32], ppps[:, 24:32], adab_pp[:, 32:40], ALU.add, [PS[6], B_pp], [B_pp])
    stt(A2, pp[:, 24:32], 1.0, pp[:, 40:48], ALU.add, ALU.mult, [B_pp], [B_A2])
    SH2 = pp[:, 16:24]

    if stop_after == "p2":
        return fin()
    B_tmpA = r_bufs[0:2] + B_tbq
    B_tmpB = r_bufs[2:4] + B_sqq + [B_rpq]
    B_oaT = [p.buf(f"oaT{g}", after=B_wring) for g in range(4)]
    NPT = 4
    ptmo = [misc_end_phaseA]
    PT = [V(172 * KB + i * KB, [512]) for i in range(NPT)]
    B_PT = [p.buf(f"PT{i}", after=B_tmpA + B_tmpB) for i in range(NPT)]
    ofp = [V(176 * KB + i * 2 * KB, [512], F32) for i in range(2)]
    B_of = [p.buf(f"of{i}", after=B_tmpA + B_tmpB) for i in range(2)]
    r0t = V(136 * KB, [512], F32)
    r1t = V(138 * KB, [512], F32)
    t1t = V(140 * KB, [512], F32)
    sqb = V(142 * KB, [512])
    rst = V(144 * KB, [512], F32)
    B_at = p.buf("attn_tmp", after=B_xs + B_xn)

    B_r0 = p.buf("attn_r0", after=[B_at])
    B_r1 = p.buf("attn_r1", after=[B_at])
    B_sq = p.buf("attn_sq", after=[B_at])
    SB = [0, 1, 7]
    NB = len(SB)
    its = []
    for u, (h, g) in enumerate([(h, g) for h in range(4) for g in range(4)]):
        nkt = 4 * g + 4
        for s_ in range(2):
            for kt in range(nkt):
                its.append((u, h, g, s_, kt, nkt))

    def geom(i):
        u, h, g, s_, kt, nkt = its[i]
        jj = max(0, kt - 4 * g)
        return u, h, g, s_, kt, nkt, jj * 128, kt >= 4 * g

    B_kz = [[p.buf(f"kz{a}{b}") for b in range(3)] for a in range(2)]
    for a in range(2):
        for b in range(3):
            zr = slice(64, 128) if a == 0 else slice(0, 64)
            memset(kzr[a][b][zr, :], 0.0, [B_kz[a][b]], eng="pool")

    def KZ(i):
        u, h, g, s_, kt, nkt, c0, diag = geom(i)
        prt = slice(s_ * 64, (s_ + 1) * 64)
        p.op("pool", lambda: nc.gpsimd.tensor_copy(out=kzr[s_][i % 3][prt, :], in_=kT[prt, h, kt * 128:(kt + 1) * 128]),
             [B_kT[kt]], [B_kz[s_][i % 3]])

    def ST(i):
        u, h, g, s_, kt, nkt, c0, diag = geom(i)
        sbi = SB[i % NB]
        mm(psum[sbi][:, c0:512], kzr[s_][i % 3], qT[:, h, g * 512 + c0:(g + 1) * 512],
           True, not diag, [B_kz[s_][i % 3]] + B_qT[4 * g + c0 // 128:4 * g + 4], [PS[sbi]])
        if diag:
            mm(psum[sbi][:, c0:c0 + 128], ident, maskb, False, True, [B_ident, B_const], [PS[sbi]])

    def EXP(i):
        u, h, g, s_, kt, nkt, c0, diag = geom(i)
        sbi = SB[i % NB]
        act(PT[i % NPT][:, c0:512], psum[sbi][:, c0:512], AF.Exp, [PS[sbi]], [B_PT[i % NPT]], scale=0.125)

    def PV(i):
        u, h, g, s_, kt, nkt, c0, diag = geom(i)
        pti = i % NPT
        mm(psum[2 + s_][:, c0:512], vA[:, kt, h * 128:(h + 1) * 128], PT[pti][:, c0:512],
           kt == 0, kt == nkt - 1, [B_vA[kt], B_PT[pti]], [PS[2 + s_]])
        mm(psum[4 + s_][:, c0:512], ones_bf, PT[pti][:, c0:512],
           kt == 0, kt == nkt - 1, [B_const, B_PT[pti]], [PS[4 + s_]])

    def E1a(u):
        recip(r0t, psum[4], [PS[4]], [B_r0])
        tt(t1t, psum[2], r0t, ALU.mult, [PS[2], B_r0], [B_r0])

    def E1b(u):
        o = ofp[u % 2]
        recip(r1t, psum[5], [PS[5]], [B_r1])
        stt(o, psum[3], nlam[:, 0:1], r1t, ALU.mult, ALU.mult, [PS[3], B_r1, B_lam], [B_of[u % 2]])
        tt(o, o, t1t, ALU.add, [B_of[u % 2], B_r0], [B_of[u % 2]])

    B_rs = p.buf("attn_rst", after=[B_at])

    def E2a(u):
        o = ofp[u % 2]
        tt(sqb, o, o, ALU.mult, [B_of[u % 2]], [B_sq], eng="pool")

    def E2b(u):
        mm(psum[6], ones_bf, sqb, True, True, [B_const, B_sq], [PS[6]])

    def E2c(u):
        act(rst, psum[6], AF.Ln, [PS[6]], [B_rs], scale=1.0 / 128, bias=epsb[:, 0:1])

    def E2d(u):
        act(rst, rst, AF.Exp, [B_rs], [B_rs], scale=-0.5)

    def E2e(u):
        h, g = u // 4, u % 4
        o = ofp[u % 2]
        stt(oaT[:, h, g * 512:(g + 1) * 512], o, sg08[:, 0:1], rst, ALU.mult, ALU.mult,
            [B_of[u % 2], B_rs, B_lam], [B_oaT[g]])
    E2_STEPS = [(8, E2a), (12, E2b), (14, E2c), (16, E2d), (18, E2e)]

    nit = len(its)
    e2_at = {}
    KZ(0)
    KZ(1)
    for i in range(nit + NB):
        if i + 2 < nit:
            KZ(i + 2)
        j = i - NB
        if j >= 0:
            EXP(j)
            PV(j)
            u, h, g, s_, kt, nkt = its[j]
            if kt == nkt - 1:
                if s_ == 0:
                    E1a(u)
                    if u == 15:
                        memset(psum[2], 0.0, [PS[2]])
                else:
                    E1b(u)
                    for off, fn_ in E2_STEPS:
                        e2_at.setdefault(j + off, []).append((fn_, u))
            for fn_, uu in e2_at.pop(j, []):
                fn_(uu)
        if i < nit:
            ST(i)
    for j in sorted(e2_at):
        for fn_, uu in e2_at[j]:
            fn_(uu)
    B_at = p.buf("attn_tmp_all", after=[B_r0, B_r1, B_sq, B_rs])
    dump("oaT", oaT, B_oaT)

    if stop_after == "p3":
        return fin()
    B_obT = [p.buf(f"obT{t}", after=B_qT) for t in range(NT)]
    scm = [V(172 * KB + i * KB, [512]) for i in range(2)]
    B_scm = [p.buf(f"scm{i}", after=B_PT) for i in range(2)]
    Uf = V(136 * KB, [4, 128], F32)
    Rf = V(138 * KB, [4, 128], F32)
    Rbf2 = [V(140 * KB + i * 512, [2, 128]) for i in range(2)]
    Rbf2 = [V(140 * KB + i * KB, [4, 128]) for i in range(2)]
    sqr = [V(142 * KB + i * KB, [512]) for i in range(2)]
    rsr = [V(144 * KB + i * 2 * KB, [512], F32) for i in range(2)]
    otr = [V(176 * KB + i * 2 * KB, [512], F32) for i in range(2)]
    B_U = p.buf("U", after=[B_at])
    B_R = [p.buf(f"Rbf{i}", after=[B_at] + B_xn) for i in range(2)]
    B_rt = [p.buf(f"ret_tmp{i}", after=[B_at] + B_of + B_xn) for i in range(2)]

    cdt = V(164 * KB, [4, 128], F32)
    B_cdt = p.buf("cdt", after=B_wring)

    def blk(h):
        return (h % 2) * 2 + h // 2
    OB = [3, 4, 5]
    SSB = [6, 7]

    def rt0(n_):
        tk = slice(n_ * 128, (n_ + 1) * 128)
        for h in (0, 2, 1, 3):
            prt = slice((h % 2) * 64, (h % 2) * 64 + 64)
            bk = h % 2
            mm(psum[bk][:, blk(h) * 128:(blk(h) + 1) * 128], rkT[prt, h // 2, tk], rqT[prt, h // 2, tk], True, True,
               [B_rkT[n_], B_rqT[n_]], [PS[bk]])
        for pr in range(2):
            p.op("pe", (lambda pr=pr: nc.tensor.matmul(
                psum[2][:, pr * 256:(pr + 1) * 256], lhsT=rktok[:, n_, pr * 128:(pr + 1) * 128],
                rhs=rv[:, n_, pr * 256:(pr + 1) * 256], start=False, stop=False, skip_group_check=True)),
                [B_rktok[n_], B_rv[n_]], [PS[2]])

    def rt1(n_):
        i = n_ % 2
        tt(scm[i][:, 0:256].rearrange("p (a b) -> p a b", b=128), psum[0][:, 0:256].rearrange("p (a b) -> p a b", b=128),
           mask01.unsqueeze(1).to_broadcast([128, 2, 128]), ALU.mult, [PS[0], B_const, B_ident], [B_scm[i]])
        tt(scm[i][:, 256:512].rearrange("p (a b) -> p a b", b=128),
           psum[1][:, 256:512].rearrange("p (a b) -> p a b", b=128),
           mask01.unsqueeze(1).to_broadcast([128, 2, 128]), ALU.mult, [PS[1], B_const, B_ident, B_scm[i]], [B_scm[i]])
        if n_ < NT - 1:
            cp(Rbf2[(n_ + 1) % 2].rearrange("p a b -> p (a b)"), psum[2], [PS[2]], [B_R[(n_ + 1) % 2]], eng="act")

    def rt2(n_):
        i = n_ % 2
        tk = slice(n_ * 128, (n_ + 1) * 128)
        b_o = OB[n_ % 3]
        for h in range(4):
            prt = slice((h % 2) * 64, (h % 2) * 64 + 64)
            mm(psum[b_o][:, h * 128:(h + 1) * 128], rv[:, n_, h * 128:(h + 1) * 128],
               scm[i][:, blk(h) * 128:(blk(h) + 1) * 128], True, n_ == 0, [B_rv[n_], B_scm[i]], [PS[b_o]])
            if n_ > 0:
                mm(psum[b_o][:, h * 128:(h + 1) * 128], Rbf2[i][prt, h, :], rqT[prt, h // 2, tk], False, True,
                   [B_R[i], B_rqT[n_]], [PS[b_o]])

    otr4 = otr + [V(164 * KB + i * 2 * KB, [512], F32) for i in range(2)]
    B_o4 = [p.buf(f"ret_o{i}", after=[B_at] + B_of + B_wring) for i in range(4)]
    B_sqr = [p.buf(f"ret_sq{i}", after=[B_at] + B_xn) for i in range(2)]

    def rt3(n_):
        i = n_ % 2
        b_o = OB[n_ % 3]
        act(sqr[i], psum[b_o], AF.Square, [PS[b_o]], [B_sqr[i]])
        cp(otr4[n_ % 4], psum[b_o], [PS[b_o]], [B_o4[n_ % 4]])

    def rt3b(n_):
        i = n_ % 2
        mm(psum[SSB[i]], ones_bf, sqr[i], True, True, [B_const, B_sqr[i]], [PS[SSB[i]]])

    def rt3c(n_):
        i = n_ % 2
        act(rsr[i], psum[SSB[i]], AF.Ln, [PS[SSB[i]]], [B_rt[i]], scale=1.0 / 128, bias=epsb[:, 0:1])
        act(rsr[i], rsr[i], AF.Exp, [B_rt[i]], [B_rt[i]], scale=-0.5)

    def rt4(n_):
        i = n_ % 2
        tk = slice(n_ * 128, (n_ + 1) * 128)
        o_ = otr4[n_ % 4]
        tt(o_, o_, rsr[i], ALU.mult, [B_o4[n_ % 4], B_rt[i]], [B_o4[n_ % 4]])
        tt(obT[:, :, tk], o_.rearrange("p (a b) -> p a b", b=128), rgsT[:, :, tk], ALU.mult,
           [B_o4[n_ % 4], B_rgs[n_ // 4]], [B_obT[n_]])

    pipeline(NT, [rt0, rt1, rt2, rt3, rt3b, rt3c, rt4])
    dump("obT", obT, B_obT)

    if stop_after == "p4":
        return fin()
    B_wga = p.buf("wga", after=B_kT)
    B_wgb = p.buf("wgb", after=B_vA)
    B_wa = p.buf("wa", after=B_rv[0:8])
    B_wb = p.buf("wb", after=B_rv[8:16])
    B_yT = [p.buf(f"yT{g}", after=B_rqT + B_rkT + B_rktok + B_rgs) for g in range(4)]
    for kc2 in range(2):
        v = win_d[:, 3072:4096].rearrange("(k p) e -> p k e", p=128)
        ld(wga[:, kc2 * 4:(kc2 + 1) * 4, :], v[:, kc2 * 4:(kc2 + 1) * 4, :], B_wga, eng="pool", join=kc2 > 0)
    for kc2 in range(2):
        v = win_d[:, 4096:5120].rearrange("(k p) e -> p k e", p=128)
        ld(wgb[:, kc2 * 4:(kc2 + 1) * 4, :], v[:, kc2 * 4:(kc2 + 1) * 4, :], B_wgb, eng="pool", join=kc2 > 0)
    ld(wab[:, 0:4, :], wa_d.rearrange("(k p) e -> p k e", p=128), B_wa, eng="pool")
    ld(wab[:, 4:8, :], wb_d.rearrange("(k p) e -> p k e", p=128), B_wb, eng="pool")
    tha = [V(136 * KB + i * 2 * KB, [512], F32) for i in range(2)]
    thb = [V(140 * KB + i * 2 * KB, [512], F32) for i in range(2)]
    pra = [V(112 * KB + i * 2 * KB, [512], F32) for i in range(2)]
    prb = [V(116 * KB + i * 2 * KB, [512], F32) for i in range(2)]
    B_yt = [p.buf(f"y_tmp{i}", after=[B_U] + B_R + B_rt + B_o4 + B_sqr + B_scm + B_rgs) for i in range(2)]
    n = 0
    for cch in range(KC):
        cs_ = slice(cch * 128, (cch + 1) * 128)
        for g in range(4):
            i = n % 2
            n += 1
            tg = slice(g * 512, (g + 1) * 512)
            b_ga, b_gb, b_ya, b_yb = (0, 1, 2, 3) if i == 0 else (4, 5, 6, 7)
            for kc in range(KC):
                mm(psum[b_ga], wga[:, kc, cs_], hT[:, kc, tg], kc == 0, kc == KC - 1,
                   [B_wga] + B_hT[4 * g:4 * g + 4], [PS[b_ga]])
            for kc in range(KC):
                mm(psum[b_gb], wgb[:, kc, cs_], hT[:, kc, tg], kc == 0, kc == KC - 1,
                   [B_wgb] + B_hT[4 * g:4 * g + 4], [PS[b_gb]])
            for hh in range(4):
                mm(psum[b_ya], wab[:, hh, cs_], oaT[:, hh, tg], hh == 0, hh == 3, [B_wa, B_oaT[g]], [PS[b_ya]])
            for hh in range(4):
                mm(psum[b_yb], wab[:, 4 + hh, cs_], obT[:, hh, tg], hh == 0, hh == 3,
                   [B_wb] + B_obT[4 * g:4 * g + 4], [PS[b_yb]])
            act(tha[i], psum[b_ga], AF.Tanh, [PS[b_ga]], [B_yt[i]], scale=0.5)
            act(thb[i], psum[b_gb], AF.Tanh, [PS[b_gb]], [B_yt[i]], scale=0.5)
            stt(pra[i], tha[i], 1.0, psum[b_ya], ALU.add, ALU.mult, [B_yt[i], PS[b_ya]], [B_yt[i]])
            stt(prb[i], thb[i], 1.0, psum[b_yb], ALU.add, ALU.mult, [B_yt[i], PS[b_yb]], [B_yt[i]])
            tt(yT[:, cch, tg], pra[i], prb[i], ALU.add, [B_yt[i]], [B_yT[g]])
    dump("yT", yT, B_yT)

    if stop_after == "p5":
        return fin()
    B_wout = p.buf("wout", after=B_wring + B_scm + B_rt + B_o4 + B_sqr + B_PT + B_of + B_tmpA + B_tmpB + [B_cdt])
    B_x1 = [p.buf(f"x1_{t}", after=B_hT + B_obT + [B_wga]) for t in range(NT)]
    B_h2T = [p.buf(f"h2T{t}", after=B_rgs + [B_wa, B_wb] + B_xs + B_xn + B_yt) for t in range(NT)]
    v = wout_d.rearrange("(k p) e -> p k e", p=128)
    ld(woutS[:, 0:4, :], v[:, 0:4, :], B_wout, eng="pool")
    ld(woutS[:, 4:8, :], v[:, 4:8, :], B_wout, eng="pool", join=True)
    for kc in range(KC):
        tt(woutS[:, kc, :], woutS[:, kc, :], g1bc, ALU.mult, [B_wout, B_g1], [B_wout], eng="pool")
    for t in range(NT):
        ld(x1[:, t, :], x_d[t * 128:(t + 1) * 128, :], B_x1[t])
    memset(ssq, 0.0, B_st)
    WB = [(0, 1), (6, 7)]

    def w0(t):
        tk = slice(t * 128, (t + 1) * 128)
        for ch in range(2):
            bi = WB[t % 2][ch]
            for kc in range(KC):
                mm(psum[bi], yT[:, kc, tk], woutS[:, kc, ch * 512:(ch + 1) * 512], kc == 0, kc == KC - 1,
                   [B_yT[t // 4], B_wout], [PS[bi]])

    def w1(t):
        for ch in range(2):
            bi = WB[t % 2][ch]
            tt(x1[:, t, ch * 512:(ch + 1) * 512], x1[:, t, ch * 512:(ch + 1) * 512], psum[bi], ALU.add,
               [B_x1[t], PS[bi]], [B_x1[t]])
    B_xnB = [p.buf(f"xnB{i}", after=[B_cs2, B_decg]) for i in range(2)]
    junkB = V(sgt_off, [D])
    B_junkB = p.buf("junkB", after=[B_trig])
    pipeline(NT, [w0, w1] + norm_stages(lambda t: x1[:, t, :], lambda t: B_x1[t], h2T, B_h2T, A2, SH2,
                                        [B_A2, B_pp], xnB, B_xnB, junkB, B_junkB))
    dump("x1", x1, B_x1)
    dump("h2T", h2T, B_h2T)

    if stop_after == "p6":
        return fin()
    GROUPS = [(0, 6), (6, 6), (12, 5), (17, 5)]
    B_act = [p.buf(f"act{i}", after=B_yT + [B_wgb]) for i in range(2)]
    B_wgu = [p.buf(f"wgu{i}", after=B_wring + B_oaT + B_xn + [B_at] + B_rt + B_R + B_sqr) for i in range(3)]
    B_wdn = [p.buf(f"wdn{i}", after=B_wring + B_oaT + B_PT + B_of + B_scm + B_rt + B_o4 + B_sqr + B_yt + B_tmpA + B_tmpB + [B_wout])
             for i in range(2)]
    sgt = [V(sgt_off + i * 2 * KB, [512], F32) for i in range(2)]
    B_sg = [p.buf(f"sg{i}", after=[B_trig, B_junkB]) for i in range(2)]
    wgu_n = [0]

    def gate_up(gi):
        j0, nj = GROUPS[gi]
        a = actS[gi % 2]
        n2 = 0
        for jj in range(nj):
            j = j0 + jj
            si = wgu_n[0] % 3
            wgu_n[0] += 1
            vg = wgu_d[:, j * 128:(j + 1) * 128].rearrange("(k p) e -> p k e", p=128)
            vu = wgu_d[:, DFF + j * 128:DFF + (j + 1) * 128].rearrange("(k p) e -> p k e", p=128)
            ld(wguS[si][:, :, 0:128], vg, B_wgu[si], eng="pool")
            ld(wguS[si][:, :, 128:256], vu, B_wgu[si], eng="pool", join=True)
            for g in range(4):
                i = n2 % 2
                n2 += 1
                tg = slice(g * 512, (g + 1) * 512)
                b_g, b_u = (4, 5) if i == 0 else (6, 7)
                for kc in range(KC):
                    mm(psum[b_g], wguS[si][:, kc, 0:128], h2T[:, kc, tg], kc == 0, kc == KC - 1,
                       [B_wgu[si]] + B_h2T[4 * g:4 * g + 4], [PS[b_g]])
                for kc in range(KC):
                    mm(psum[b_u], wguS[si][:, kc, 128:256], h2T[:, kc, tg], kc == 0, kc == KC - 1,
                       [B_wgu[si]] + B_h2T[4 * g:4 * g + 4], [PS[b_u]])
                act(sgt[i], psum[b_g], AF.Silu, [PS[b_g]], [B_sg[i]])
                tt(a[:, jj, tg], sgt[i], psum[b_u], ALU.mult, [B_sg[i], PS[b_u]], [B_act[gi % 2]])

    def down(gi):
        j0, nj = GROUPS[gi]
        a = actS[gi % 2]
        wi = gi % 2
        vd = wdn_d[j0 * 128:(j0 + nj) * 128, :].rearrange("(k p) e -> p k e", p=128)
        ld(wdnS[wi][:, 0:nj, :], vd, B_wdn[wi], eng="pool")
        for jj in range(nj):
            tt(wdnS[wi][:, jj, :], wdnS[wi][:, jj, :], g2bc, ALU.mult, [B_wdn[wi], B_g2], [B_wdn[wi]], eng="pool")
        for t in range(NT):
            tk = slice(t * 128, (t + 1) * 128)
            for ch in range(2):
                bi = (2 * t + ch) % 4
                for jj in range(nj):
                    mm(psum[bi], a[:, jj, tk], wdnS[wi][:, jj, ch * 512:(ch + 1) * 512], jj == 0, jj == nj - 1,
                       [B_act[gi % 2], B_wdn[wi]], [PS[bi]])
                tt(x1[:, t, ch * 512:(ch + 1) * 512], x1[:, t, ch * 512:(ch + 1) * 512], psum[bi], ALU.add,
                   [B_x1[t], PS[bi]], [B_x1[t]])
            if gi == len(GROUPS) - 1:
                st(out_d[tk, :], x1[:, t, :], B_x1[t])

    gate_up(0)
    gate_up(1)
    down(0)
    gate_up(2)
    down(1)
    gate_up(3)
    down(2)
    down(3)

    return fin()


def _consts():
    f32 = np.float32
    half_a = 8
    inv_a = (f32(500000.0) ** (-(np.arange(half_a, dtype=f32)) / f32(half_a))).astype(f32)
    half_r = 32
    inv_r = (f32(10000.0) ** (-(np.arange(half_r, dtype=f32)) / f32(half_r))).astype(f32)
    ia64 = 500000.0 ** (-(np.arange(half_a, dtype=np.float64)) / half_a)
    ir64 = 10000.0 ** (-(np.arange(half_r, dtype=np.float64)) / half_r)
    i64 = np.concatenate([ia64, ir64])
    hi = i64.astype(f32)
    lo = (i64 - hi.astype(np.float64)).astype(f32)
    invf = np.concatenate([hi, lo])[None, :].astype(f32)
    lg = np.log(1.0 - 2.0 ** (-5.0 - np.arange(4, dtype=np.float64)))
    tpos = (np.arange(NT, dtype=np.float64)[None, :, None] * 128.0 + np.arange(128, dtype=np.float64)[:, None, None] + 1.0)
    qdec = np.exp(lg[None, None, :] * tpos)
    kdec = np.exp(-lg[None, None, :] * tpos) * (64.0 ** -0.5)
    dec = np.concatenate([qdec, kdec], axis=2).reshape(128, NT * 8).astype(f32)
    ident = np.eye(128, dtype=f32)
    k = np.arange(128)[:, None]
    q = np.arange(128)[None, :]
    mask01 = (q >= k).astype(f32)
    maskb = np.where(q >= k, 0.0, NEG).astype(f32)
    return dict(k_invf=invf, k_dec=dec, k_ident=ident, k_maskb=maskb, k_mask01=mask01)


def make_in_maps(inputs):
    x = np.asarray(inputs["x"], dtype=np.float32)
    c = np.asarray(inputs["c"], dtype=np.float32)
    pos = np.asarray(inputs["positions"], dtype=np.int32)
    shared = {}
    for k in ("ada_w", "w_in", "w_branch_a", "w_branch_b", "w_out", "w_gate_up", "w_down"):
        shared[k] = np.ascontiguousarray(np.asarray(inputs[k], dtype=np.float32)[0])
    for k in ("ada_b", "norm1_g", "q_norm_g", "k_norm_g", "lambda_q1", "lambda_k1", "lambda_q2", "lambda_k2",
              "subln_g", "norm2_g"):
        shared[k] = np.ascontiguousarray(np.asarray(inputs[k], dtype=np.float32)[0][None, :])
    shared.update(_consts())
    maps = []
    for b in range(8):
        m = dict(shared)
        m["x"] = np.ascontiguousarray(x[b])
        m["c"] = np.ascontiguousarray(c[b].reshape(KC, 128).T)
        m["pos"] = np.ascontiguousarray(pos[b].reshape(NT, 128).T)
        maps.append(m)
    return maps


def kernel(**inputs):
    nc, _ = build_nc()
    maps = make_in_maps(inputs)
    res = run_bass_kernel_spmd(nc, maps, core_ids=list(range(8)))
    out = np.stack([np.asarray(res.results[b]["out"], dtype=np.float32).reshape(S, D) for b in range(8)], axis=0)
    return out
```
